# Optimizing a Trainium2 kernel written in Bass

```python
import jax, jax.numpy as jnp
from jax import lax
import numpy as np

D_MODEL = 1024
BATCH = 8
SEQ = 4096
DEPTH = 2

N_EVEN = (DEPTH + 1) // 2
N_ODD = DEPTH // 2
EPS = 1e-6
A_WIDTH = D_MODEL // 2
CONV_WIDTH = 3
B_WIDTH = D_MODEL // 2
HG_HEADS = 4
HG_DK = B_WIDTH // HG_HEADS
HG_DV = B_WIDTH // HG_HEADS
HG_CHUNK = 64
IN0_SIZES = (A_WIDTH, A_WIDTH, A_WIDTH, B_WIDTH, B_WIDTH, B_WIDTH, B_WIDTH)
IN0_COLS = sum(IN0_SIZES)
IN0_SPLITS = tuple(int(s) for s in np.cumsum(IN0_SIZES)[:-1])
GM_WIDTH = D_MODEL
GM_GROUPS = 4
GM_CHUNK = 128
D_FF = 4 * D_MODEL

kernel_name = "hybrid_conv_hgrn2_gmlp_adaln"


def rmsnorm(x, g):
    xf = x.astype(jnp.float32)
    inv = lax.rsqrt(jnp.mean(xf * xf, axis=-1, keepdims=True) + EPS)
    return (xf * inv).astype(x.dtype) * g


def layernorm(x, g, b):
    xf = x.astype(jnp.float32)
    mu = jnp.mean(xf, axis=-1, keepdims=True)
    xc = xf - mu
    inv = lax.rsqrt(jnp.mean(xc * xc, axis=-1, keepdims=True) + EPS)
    return (xc * inv).astype(x.dtype) * g + b


def ada_modulation(c, w, b):
    m = jnp.einsum('bd,de->be', jax.nn.silu(c), w) + b
    return jnp.split(m[:, None, :], 6, axis=-1)


def short_conv_mixer(gate_b, gate_c, h, w_conv, b_conv):
    z = gate_c * h
    s = z.shape[1]
    zp = jnp.pad(z, ((0, 0), (CONV_WIDTH - 1, 0), (0, 0)))
    conv = b_conv
    for tap in range(CONV_WIDTH):
        conv = conv + zp[:, tap:tap + s, :] * w_conv[tap]
    return gate_b * conv


def hgrn2_mixer(q, f_logit, i, g, lower_bound, gain):
    f32 = jnp.float32
    bsz, s, _ = q.shape
    n_chunks = s // HG_CHUNK
    lb = lower_bound.astype(f32)
    f = lb + (1.0 - lb) * jax.nn.sigmoid(f_logit.astype(f32))
    log_f = jnp.log(f)
    k = 1.0 - f

    def to_chunks(t, d):
        return t.reshape(bsz, n_chunks, HG_CHUNK, HG_HEADS, d).transpose(1, 0, 3, 2, 4)

    qc = to_chunks(q.astype(f32), HG_DK)
    kc = to_chunks(k, HG_DK)
    lfc = to_chunks(log_f, HG_DK)
    vc = to_chunks(i.astype(f32), HG_DV)
    causal = jnp.tril(jnp.ones((HG_CHUNK, HG_CHUNK), dtype=bool))[:, :, None]

    def step(state, inp):
        qb, kb, lfb, vb = inp
        cum = jnp.cumsum(lfb, axis=2)
        o_inter = jnp.einsum('bhck,bhkv->bhcv', qb * jnp.exp(cum), state)
        diff = cum[:, :, :, None, :] - cum[:, :, None, :, :]
        decay = jnp.exp(jnp.where(causal, diff, -jnp.inf))
        scores = jnp.einsum('bhtk,bhtsk,bhsk->bhts', qb, decay, kb)
        o_intra = jnp.einsum('bhts,bhsv->bhtv', scores, vb)
        last = cum[:, :, -1:, :]
        k_dec = kb * jnp.exp(last - cum)
        new_state = (jnp.exp(last[:, :, 0, :])[..., None] * state
                     + jnp.einsum('bhck,bhcv->bhkv', k_dec, vb))
        return new_state, o_inter + o_intra

    s0 = jnp.zeros((bsz, HG_HEADS, HG_DK, HG_DV), f32)
    _, o = lax.scan(step, s0, (qc, kc, lfc, vc))
    o = o.transpose(1, 0, 3, 2, 4).reshape(bsz, s, HG_HEADS, HG_DV)
    o = o * lax.rsqrt(jnp.mean(o * o, axis=-1, keepdims=True) + EPS)
    o = o.reshape(bsz, s, HG_HEADS * HG_DV) * gain.astype(f32) * jax.nn.silu(g.astype(f32))
    return o.astype(q.dtype)


def spatial_gating_mixer(z, ln_g, ln_b, w_s, b_s):
    u, v = jnp.split(z, 2, axis=-1)
    v = layernorm(v, ln_g, ln_b)
    bsz, s, e = v.shape
    vc = v.reshape(bsz, s // GM_CHUNK, GM_CHUNK, GM_GROUPS, e // GM_GROUPS)
    mask = jnp.tril(jnp.ones((GM_CHUNK, GM_CHUNK), dtype=w_s.dtype))
    w = w_s * mask
    mixed = jnp.einsum('gts,bnsgd->bntgd', w, vc) + b_s.T[None, None, :, :, None]
    return u * mixed.reshape(bsz, s, e)


def setup_inputs(seed: int = 0) -> dict:
    key = jax.random.key(seed)
    ks = jax.random.split(key, 24)
    nrm = jax.random.normal
    D = D_MODEL
    inp = {}
    inp["x"] = nrm(ks[0], (BATCH, SEQ, D), jnp.float32)
    inp["c"] = nrm(ks[1], (BATCH, D), jnp.float32)
    inp["ada_w"] = nrm(ks[2], (DEPTH, D, 6 * D), jnp.float32) * D ** -0.5
    inp["ada_b"] = nrm(ks[3], (DEPTH, 6 * D), jnp.float32) * 0.02
    inp["norm_mix_g"] = 1.0 + 0.02 * nrm(ks[4], (DEPTH, D), jnp.float32)
    inp["norm_ffn_g"] = 1.0 + 0.02 * nrm(ks[5], (DEPTH, D), jnp.float32)
    inp["w_in0"] = nrm(ks[6], (N_EVEN, D, IN0_COLS), jnp.float32) * D ** -0.5
    inp["conv_w"] = nrm(ks[7], (N_EVEN, CONV_WIDTH, A_WIDTH), jnp.float32) * CONV_WIDTH ** -0.5
    inp["conv_b"] = 0.02 * nrm(ks[8], (N_EVEN, A_WIDTH), jnp.float32)
    inp["hg_lb"] = 0.5 * nrm(ks[9], (DEPTH + 1, B_WIDTH), jnp.float32)
    inp["hg_gain"] = 1.0 + 0.02 * nrm(ks[10], (N_EVEN, B_WIDTH), jnp.float32)
    inp["w_out0"] = nrm(ks[11], (N_EVEN, A_WIDTH + B_WIDTH, D), jnp.float32) * (A_WIDTH + B_WIDTH) ** -0.5
    inp["w_in1"] = nrm(ks[12], (N_ODD, D, 2 * GM_WIDTH), jnp.float32) * D ** -0.5
    inp["b_in1"] = 0.02 * nrm(ks[13], (N_ODD, 2 * GM_WIDTH), jnp.float32)
    inp["gm_ln_g"] = 1.0 + 0.02 * nrm(ks[14], (N_ODD, GM_WIDTH), jnp.float32)
    inp["gm_ln_b"] = 0.02 * nrm(ks[15], (N_ODD, GM_WIDTH), jnp.float32)
    inp["gm_ws"] = nrm(ks[16], (N_ODD, GM_GROUPS, GM_CHUNK, GM_CHUNK), jnp.float32) * GM_CHUNK ** -0.5
    inp["gm_bs"] = 1.0 + 0.02 * nrm(ks[17], (N_ODD, GM_GROUPS, GM_CHUNK), jnp.float32)
    inp["w_out1"] = nrm(ks[18], (N_ODD, GM_WIDTH, D), jnp.float32) * GM_WIDTH ** -0.5
    inp["w_ff1"] = nrm(ks[19], (DEPTH, D, D_FF), jnp.float32) * D ** -0.5
    inp["w_ff2"] = nrm(ks[20], (DEPTH, D_FF, D), jnp.float32) * D_FF ** -0.5
    inp["final_g"] = 1.0 + 0.02 * nrm(ks[21], (D,), jnp.float32)
    return inp


def reference(x, c, ada_w, ada_b, norm_mix_g, norm_ffn_g, w_in0, conv_w, conv_b,
              hg_lb, hg_gain, w_out0, w_in1, b_in1, gm_ln_g, gm_ln_b, gm_ws, gm_bs,
              w_out1, w_ff1, w_ff2, final_g):
    lower_bounds = jnp.cumsum(jax.nn.softmax(hg_lb.astype(jnp.float32), axis=0), axis=0)
    for layer in range(DEPTH):
        sh1, sc1, g1, sh2, sc2, g2 = ada_modulation(c, ada_w[layer], ada_b[layer])
        h = rmsnorm(x, norm_mix_g[layer]) * (1.0 + sc1) + sh1
        j = layer // 2
        if layer % 2 == 0:
            p = jnp.einsum('bsd,de->bse', h, w_in0[j])
            a_b, a_c, a_h, b_q, b_f, b_i, b_g = jnp.split(p, IN0_SPLITS, axis=-1)
            y_a = short_conv_mixer(a_b, a_c, a_h, conv_w[j], conv_b[j])
            y_b = hgrn2_mixer(b_q, b_f, b_i, b_g, lower_bounds[layer], hg_gain[j])
            y = jnp.einsum('bse,ed->bsd', jnp.concatenate([y_a, y_b], axis=-1), w_out0[j])
        else:
            z = jax.nn.gelu(jnp.einsum('bsd,de->bse', h, w_in1[j]) + b_in1[j])
            y = spatial_gating_mixer(z, gm_ln_g[j], gm_ln_b[j], gm_ws[j], gm_bs[j])
            y = jnp.einsum('bse,ed->bsd', y, w_out1[j])
        x = x + g1 * y
        h = rmsnorm(x, norm_ffn_g[layer]) * (1.0 + sc2) + sh2
        hid = jnp.square(jax.nn.relu(jnp.einsum('bsd,df->bsf', h, w_ff1[layer])))
        x = x + g2 * jnp.einsum('bsf,fd->bsd', hid, w_ff2[layer])
    return rmsnorm(x, final_g)
```

```python
import contextlib
import numpy as np
import concourse.bass as bass
import concourse.mybir as mybir
from concourse.bass_utils import run_bass_kernel_spmd

F32 = mybir.dt.float32
BF16 = mybir.dt.bfloat16
AF = mybir.ActivationFunctionType
ALU = mybir.AluOpType

D = 1024
S = 4096
TT = 512
NT = S // TT
EPS = 1e-6
NCORES = 8

V_C = 0
V_ADAB = 8
V_NMG = 104
V_NFG = 120
V_CONVW = 136
V_CONVB = 148
V_HGLB = 152
V_GAIN = 164
V_BIN1U = 168
V_FING = 176
V_LNG = 184
V_LNB = 192
NVEC = 256
R_BV = 0
R_LNB = 1024
R_BS = 2048
R_FG = 2560
NROW = 3584
C_ID = 0
C_MBD = 128
C_TRIL = 256
C_CMASK = 384
NCONST = 896


import os
CUT = 0


class StopBuild(Exception):
    pass


def cut(n):
    if CUT == n:
        raise StopBuild()


class Buf:
    __slots__ = ("name", "w", "r")

    def __init__(self, name):
        self.name = name
        self.w = None
        self.r = {}


class DSem:
    def __init__(self, handle):
        self.h = handle
        self.count = 0


class Prog:
    def __init__(self, nc, stack):
        self.nc = nc
        self.stack = stack
        self.q = {}
        for name in ("pe", "act", "dve", "pool", "sp"):
            sem = stack.enter_context(nc.semaphore("q_" + name))
            self.q[name] = {"th": [], "sem": sem, "count": 0, "waited": {}}
        self.dsems = []
        self.nds = 0

    def dsem(self):
        self.nds += 1
        d = DSem(self.stack.enter_context(self.nc.semaphore("d%d" % self.nds)))
        self.dsems.append(d)
        return d

    def _wait(self, qn, tok):
        q = self.q[qn]
        sem, val = tok
        if q["waited"].get(id(sem), 0) >= val:
            return
        if qn == "pe" and sem is q["sem"]:
            return
        q["waited"][id(sem)] = val
        q["th"].append(("w", sem, val))

    def op(self, qn, fn, reads=(), writes=(), dsem=None):
        q = self.q[qn]
        for b in reads:
            if b.w is not None:
                self._wait(qn, b.w)
        for b in writes:
            for t in b.r.values():
                self._wait(qn, t)
            if b.w is not None:
                self._wait(qn, b.w)
        if dsem is None:
            q["count"] += 1
            tok = (q["sem"], q["count"])
            q["th"].append(("o", fn, q["sem"], 1))
        else:
            dsem.count += 16
            tok = (dsem.h, dsem.count)
            q["th"].append(("o", fn, dsem.h, 16))
        for b in writes:
            b.w = tok
            b.r = {}
        for b in reads:
            if b.w is not tok:
                b.r[id(tok[0])] = tok
        return tok

    def barrier(self, exclude=()):
        toks = []
        ex = set(id(d) for d in exclude)
        for qn, q in self.q.items():
            if q["count"] > 0:
                toks.append((q["sem"], q["count"]))
        for d in self.dsems:
            if id(d) in ex:
                continue
            if d.count > 0:
                toks.append((d.h, d.count))
        for qn in ("pe", "act", "dve", "pool", "sp"):
            for t in toks:
                self._wait(qn, t)

    def emit(self, qn, eng):
        for th in self.q[qn]["th"]:
            if th[0] == "w":
                eng.wait_ge(th[1], th[2])
            else:
                ins = th[1](eng)
                ins.then_inc(th[2], th[3])


def build_program(stop_after=None, ntiles=NT):
    nc = bass.Bass("TRN2", target_bir_lowering=False)
    dt = nc.dram_tensor
    x_d = dt("x", [S, D], F32, kind="ExternalInput").ap()
    vecs_d = dt("vecs", [NVEC, 128], F32, kind="ExternalInput").ap()
    rows_d = dt("rows", [1, NROW], F32, kind="ExternalInput").ap()
    consts_d = dt("consts", [128, NCONST], F32, kind="ExternalInput").ap()
    adaw_d = dt("ada_w", [2, D, 6 * D], F32, kind="ExternalInput").ap()
    win0_d = dt("w_in0", [D, 3584], F32, kind="ExternalInput").ap()
    wout0_d = dt("w_out0", [D, D], F32, kind="ExternalInput").ap()
    win1_d = dt("w_in1", [D, 2048], F32, kind="ExternalInput").ap()
    wout1_d = dt("w_out1", [D, D], F32, kind="ExternalInput").ap()
    wff1_d = dt("w_ff1", [2, D, 4096], F32, kind="ExternalInput").ap()
    wff2_d = dt("w_ff2", [2, 4096, D], F32, kind="ExternalInput").ap()
    gmws_d = dt("gm_ws", [4, 128, 128], F32, kind="ExternalInput").ap()
    out_d = dt("out", [S, D], F32, kind="ExternalOutput").ap()
    xs_d = dt("xs", [8, 128, S], F32, kind="Internal").ap()

    phases = ["M0", "F0", "M1", "F1"]
    if stop_after is not None:
        phases = phases[: phases.index(stop_after) + 1]
    do_final_norm = stop_after is None

    with contextlib.ExitStack() as stack:
        P = Prog(nc, stack)
        sb = lambda name, shape, dtype: stack.enter_context(nc.sbuf_tensor("sb_" + name, shape, dtype))
        consts = sb("consts", [128, NCONST], F32)
        vcol = sb("vcol", [128, NVEC], F32)
        identb = sb("identb", [128, 128], BF16)
        onesD = sb("onesD", [128, 128], BF16)
        ones128 = sb("ones128", [128, 128], BF16)
        maskbd = sb("maskbd", [128, 128], F32)
        modc = sb("modc", [128, 96], F32)
        s1c = sb("s1c", [128, 32], F32)
        lbc = sb("lbc", [128, 16], F32)
        epsc = sb("epsc", [128, 1], F32)
        NA = 51400
        arena = sb("arena", [128, NA], F32)
        aoff = [0]

        def areset():
            aoff[0] = 0

        def aalloc(name, shape, dtype):
            n = 1
            for d_ in shape[1:]:
                n *= d_
            words = n if dtype == F32 else (n + 1) // 2
            words = (words + 7) // 8 * 8
            o = aoff[0]
            assert o + words <= NA, (name, o, words)
            aoff[0] = o + words
            v = arena[0:shape[0], o:o + words]
            if dtype != F32:
                v = v.bitcast(dtype)
            v = v[:, 0:n]
            if len(shape) == 3:
                v = v.rearrange("p (a b) -> p a b", b=shape[2])
            elif len(shape) == 4:
                v = v.rearrange("p (a b c) -> p a b c", b=shape[2], c=shape[3])
            return v
        ps = stack.enter_context(nc.psum_tensor("ps", [128, 7, 512], F32))
        psb = stack.enter_context(nc.psum_tensor("psb", [128, 1024], BF16))
        bank = [Buf("bank%d" % i) for i in range(7)]
        bankb = Buf("bankb")
        b_consts, b_vcol, b_rows, b_small = Buf("consts"), Buf("vcol"), Buf("rows"), Buf("small")
        ident = consts[:, C_ID:C_ID + 128]
        tril = consts[:, C_TRIL:C_TRIL + 128]
        cmask = consts[:, C_CMASK:C_CMASK + 512]
        xs_bufs = [Buf("xs%d" % i) for i in range(NT)]

        def rsqrt_eps(out_ap, out_buf, in_ap, in_buf):
            P.op("act", lambda e: e.activation(out=out_ap, in_=in_ap, func=AF.Ln, bias=epsc[0:out_ap.shape[0], :], scale=1.0),
                 reads=[in_buf, b_small], writes=[out_buf])
            P.op("act", lambda e: e.activation(out=out_ap, in_=out_ap, func=AF.Exp, scale=-0.5),
                 reads=[out_buf], writes=[out_buf])

        def mm(e, out, lhsT, rhs, start, stop):
            return e.matmul(out, lhsT, rhs, start=start, stop=stop)

        d_c = P.dsem()
        P.op("sp", lambda e: e.dma_start(out=consts[:], in_=consts_d[:, :]), writes=[b_consts], dsem=d_c)
        if True:
            areset()
            vrow = aalloc("vrow", [128, 2, 128], F32)
            adaring = aalloc("adaring", [128, 3, 8, 512], BF16)
            siluc = aalloc("siluc", [128, 8], BF16)
            etmp = aalloc("etmp", [128, 16], F32)
            b_vrow = Buf("vrow")
            d_v = P.dsem()
            P.op("sp", lambda e: e.dma_start(out=vrow[:], in_=vecs_d.rearrange("(t p) f -> p t f", p=128)),
                 writes=[b_vrow], dsem=d_v)
            P.op("dve", lambda e: e.tensor_copy(out=identb[:], in_=ident), reads=[b_consts], writes=[b_small])
            P.op("dve", lambda e: e.memset(epsc[:], EPS), writes=[b_small])
            P.op("dve", lambda e: e.memset(onesD[:], 1.0 / D), writes=[b_small])
            P.op("dve", lambda e: e.memset(ones128[:], 1.0 / 128), writes=[b_small])
            P.op("dve", lambda e: e.tensor_copy(out=maskbd[:], in_=consts[:, C_MBD:C_MBD + 128]),
                 reads=[b_consts], writes=[b_small])
            for t in range(2):
                P.op("pe", lambda e, t=t: e.transpose(out=ps[:, t, 0:128], in_=vrow[:, t, :], identity=ident),
                     reads=[b_vrow, b_consts], writes=[bank[t]])
                P.op("dve", lambda e, t=t: e.tensor_copy(out=vcol[:, t * 128:(t + 1) * 128], in_=ps[:, t, 0:128]),
                     reads=[bank[t]], writes=[b_vcol])
            P.op("act", lambda e: e.activation(out=siluc[:], in_=vcol[:, V_C:V_C + 8], func=AF.Silu),
                 reads=[b_vcol], writes=[b_small])
            P.op("act", lambda e: e.activation(out=etmp[:, 0:12], in_=vcol[:, V_HGLB:V_HGLB + 12], func=AF.Exp),
                 reads=[b_vcol], writes=[b_small])
            P.op("dve", lambda e: e.tensor_tensor(out=etmp[:, 12:16], in0=etmp[:, 0:4], in1=etmp[:, 4:8], op=ALU.add),
                 reads=[b_small], writes=[b_small])
            P.op("dve", lambda e: e.tensor_tensor(out=etmp[:, 12:16], in0=etmp[:, 12:16], in1=etmp[:, 8:12], op=ALU.add),
                 reads=[b_small], writes=[b_small])
            P.op("dve", lambda e: e.reciprocal(out=etmp[:, 12:16], in_=etmp[:, 12:16]), reads=[b_small], writes=[b_small])
            P.op("dve", lambda e: e.tensor_tensor(out=lbc[:, 0:4], in0=etmp[:, 0:4], in1=etmp[:, 12:16], op=ALU.mult),
                 reads=[b_small], writes=[b_small])
            P.op("dve", lambda e: e.tensor_scalar(out=lbc[:, 4:8], in0=lbc[:, 0:4], scalar1=-1.0, scalar2=1.0,
                                                  op0=ALU.mult, op1=ALU.add), reads=[b_small], writes=[b_small])
            P.op("dve", lambda e: e.tensor_scalar(out=lbc[:, 8:12], in0=lbc[:, 4:8], scalar1=-1.0, scalar2=None,
                                                  op0=ALU.mult), reads=[b_small], writes=[b_small])
            aslots = [Buf("ada%d" % i) for i in range(3)]
            adsem = [P.dsem() for _ in range(3)]
            b_ada = bank[2]
            n = 0
            for l in range(2):
                wv = adaw_d[l].rearrange("(kc p) e -> p kc e", p=128)
                for ch in range(12):
                    sl = n % 3
                    P.op("pool", lambda e, sl=sl, wv=wv, ch=ch: e.dma_start(
                        out=adaring[:, sl], in_=wv[:, :, ch * 512:(ch + 1) * 512]),
                        writes=[aslots[sl]], dsem=adsem[sl])

                    def f(e, sl=sl, col0=l * 48 + ch * 4):
                        ins = None
                        for es in range(4):
                            for kc in range(8):
                                ins = mm(e, ps[:, 2, col0 + es:col0 + es + 1],
                                         adaring[:, sl, kc, es * 128:(es + 1) * 128], siluc[:, kc:kc + 1],
                                         kc == 0, kc == 7)
                        return ins
                    P.op("pe", f, reads=[aslots[sl], b_small], writes=[b_ada])
                    n += 1
            P.op("dve", lambda e: e.tensor_tensor(out=modc[:], in0=ps[:, 2, 0:96], in1=vcol[:, V_ADAB:V_ADAB + 96],
                                                  op=ALU.add), reads=[b_ada, b_vcol], writes=[b_small])
            for l in range(2):
                P.op("dve", lambda e, l=l: e.scalar_tensor_tensor(
                    out=s1c[:, l * 16:l * 16 + 8], in0=modc[:, l * 48 + 8:l * 48 + 16], scalar=1.0,
                    in1=vcol[:, V_NMG + l * 8:V_NMG + l * 8 + 8], op0=ALU.add, op1=ALU.mult),
                    reads=[b_small, b_vcol], writes=[b_small])
                P.op("dve", lambda e, l=l: e.scalar_tensor_tensor(
                    out=s1c[:, l * 16 + 8:l * 16 + 16], in0=modc[:, l * 48 + 32:l * 48 + 40], scalar=1.0,
                    in1=vcol[:, V_NFG + l * 8:V_NFG + l * 8 + 8], op0=ALU.add, op1=ALU.mult),
                    reads=[b_small, b_vcol], writes=[b_small])
            P.barrier()

        def mod(l, grp, c):
            return modc[:, l * 48 + grp * 8 + c:l * 48 + grp * 8 + c + 1]

        def scol(l, which, c):
            o = l * 16 + which * 8 + c
            return s1c[:, o:o + 1]

        def vc(base, i):
            return vcol[:, base + i:base + i + 1]

        wslot = [Buf("wslot%d" % k) for k in range(16)]
        wdsem = [P.dsem() for _ in range(16)]
        wtoks = []
        wissued = set()

        def slot_view(k, kind):
            v = arena[:, k * 2048:(k + 1) * 2048].bitcast(BF16)
            return v.rearrange("p (a b) -> p a b", b=(512 if kind == "k" else D))

        def phase_chunks(ph):
            out = []
            if ph == "M0":
                wv = win0_d.rearrange("(kc p) e -> p kc e", p=128)
                wov = wout0_d.rearrange("(kc p) e -> p kc e", p=128)
                for i in range(7):
                    out.append((("M0", "in", i), i, "k", wv[:, :, i * 512:(i + 1) * 512]))
                for i in range(2):
                    out.append((("M0", "out", i), 7 + i, "k", wov[:, :, i * 512:(i + 1) * 512]))
            elif ph == "M1":
                wv = win1_d.rearrange("(kc p) e -> p kc e", p=128)
                wov = wout1_d.rearrange("(kc p) e -> p kc e", p=128)
                for i in range(4):
                    out.append((("M1", "in", i), i, "k", wv[:, :, i * 512:(i + 1) * 512]))
                for i in range(2):
                    out.append((("M1", "out", i), 4 + i, "k", wov[:, :, i * 512:(i + 1) * 512]))
            else:
                lf = int(ph[1])
                w1v = wff1_d[lf].rearrange("(kc p) e -> p kc e", p=128)
                w2v = wff2_d[lf].rearrange("(fc p) d -> p fc d", p=128)
                for i in range(8):
                    out.append(((ph, "w1", i), i, "k", w1v[:, :, i * 512:(i + 1) * 512]))
                for i in range(8):
                    out.append(((ph, "w2", i), 8 + i, "w", w2v[:, i * 4:(i + 1) * 4, :]))
            return out

        def issue_weights(ph, max_slot):
            for key, k, kind, src in phase_chunks(ph):
                if k >= max_slot or key in wissued:
                    continue
                wissued.add(key)
                if len(wtoks) >= 4:
                    P._wait("pool", wtoks[-4])
                wtoks.append(P.op("pool", lambda e, k=k, kind=kind, src=src: e.dma_start(out=slot_view(k, kind), in_=src),
                                  writes=[wslot[k]], dsem=wdsem[k]))

        def load_weight_chunks(wsb, src_fn, nchunks, name):
            bufs = []
            toks = []
            for i in range(nchunks):
                b = Buf("%s%d" % (name, i))
                d = P.dsem()
                if i >= 4:
                    P._wait("pool", toks[i - 4])
                toks.append(P.op("pool", lambda e, i=i: e.dma_start(out=wsb[:, i], in_=src_fn(i)), writes=[b], dsem=d))
                bufs.append(b)
            return bufs

        def norm_and_modulate(xT, xTb, sqt, sqb, tmp, tmpb, hT, hb, invB, b_inv, b_norm_bank, nbank, l, which,
                              sq_src=None):
            for c in range(8):
                src_ap, src_b = (xT[:, c, :], xTb[c]) if sq_src is None else sq_src[c]
                P.op("act", lambda e, c=c, src_ap=src_ap: e.activation(out=sqt[:, c % 2, :], in_=src_ap, func=AF.Square),
                     reads=[src_b], writes=[sqb[c % 2]])
                P.op("pe", lambda e, c=c: mm(e, ps[:, nbank, :], onesD[:], sqt[:, c % 2, :], c == 0, c == 7),
                     reads=[sqb[c % 2], b_small], writes=[b_norm_bank])
            rsqrt_eps(invB[:], b_inv, ps[:, nbank, :], b_norm_bank)
            if hT is None:
                return
            shift_grp = 0 if which == 0 else 3
            ntmp = tmp.shape[1]
            for c in range(8):
                P.op("dve", lambda e, c=c: e.scalar_tensor_tensor(
                    out=tmp[:, c % ntmp, :], in0=xT[:, c, :], scalar=scol(l, which, c), in1=invB[:],
                    op0=ALU.mult, op1=ALU.mult), reads=[xTb[c], b_inv, b_small], writes=[tmpb[c % ntmp]])
                P.op("act", lambda e, c=c: e.activation(out=hT[:, c, :], in_=tmp[:, c % ntmp, :], func=AF.Identity,
                                                        bias=mod(l, shift_grp, c), scale=1.0),
                     reads=[tmpb[c % ntmp], b_small], writes=[hb[c]])

        def store_xs(xT, xTb, ti, dsem):
            P.op("sp", lambda e: e.dma_start(out=xs_d[:, :, ti * TT:(ti + 1) * TT].rearrange("c p t -> p c t"),
                                             in_=xT[:]), reads=xTb, writes=[xs_bufs[ti]], dsem=dsem)

        def load_xs(xT, xTb, ti, dsem):
            P.op("sp", lambda e: e.dma_start(out=xT[:], in_=xs_d[:, :, ti * TT:(ti + 1) * TT].rearrange("c p t -> p c t")),
                 reads=[xs_bufs[ti]], writes=xTb, dsem=dsem)

        def phase_m0(next_ph=None):
            l = 0
            with contextlib.ExitStack() as st:
                areset()
                a = aalloc
                win = a("m0_win", [128, 7, 8, 512], BF16)
                wout = a("m0_wout", [128, 2, 8, 512], BF16)
                xin_raw = a("m0_xin", [128, 4 * D], F32)
                xin = xin_raw.rearrange("p (g d) -> p g d", d=D)
                ftB = xin_raw[:, 0:3072].rearrange("p (a b) -> p a b", b=TT)
                kdB = xin_raw[:, 3072:3328].bitcast(BF16).rearrange("p (a b) -> p a b", b=TT)
                cref2B = xin_raw[:, 3328:3344].rearrange("p (a b) -> p a b", b=2)
                Rg = [a("m0_R0", [128, 8, TT], F32), a("m0_R1", [128, 8, TT], F32)]
                RgB = [[Buf("R%d_%d" % (r_, c)) for c in range(8)] for r_ in range(2)]
                sqt = a("m0_sq", [128, 2, TT], BF16)
                tmp = a("m0_tmp", [128, 1, TT], F32)
                hT = a("m0_h", [128, 8, TT], BF16)
                invB = a("m0_inv", [128, TT], F32)
                ac = a("m0_ac", [128, 2, TT], BF16)
                z = a("m0_z", [128, 4, TT + 2], F32)

                ycat = a("m0_ycat", [128, 8, TT], BF16)

                kA = a("m0_kA", [128, 4, TT], BF16)
                kB = a("m0_kB", [128, 4, TT], BF16)
                qs = a("m0_qs", [128, 4, TT], BF16)
                cref2 = a("m0_cref2", [128, 8, 2], F32)
                kd = a("m0_kd", [128, 1, TT], BF16)
                kdT = a("m0_kdT", [128, 4, 512], BF16)
                qe = a("m0_qe", [128, 4, TT], BF16)
                vt = a("m0_v", [128, 4, 512], BF16)
                sg = a("m0_sg", [128, 4, TT], BF16)
                elb = a("m0_el", [128, 4, 8], F32)
                scT = a("m0_scT", [128, 2, 4, 128], BF16)
                osq = a("m0_osq", [128, 2, 512], BF16)
                invo = a("m0_invo", [128, 512], F32)
                t1 = a("m0_t1", [128, 512], F32)
                Sst = a("m0_S", [128, 512], F32)
                Sb = a("m0_Sb", [128, 3, 512], BF16)
                oi = a("m0_oi", [128, 512], F32)
                osum = a("m0_osum", [128, 2, 512], F32)
                b_oi = Buf("oi"); b_osum = [Buf("osum0"), Buf("osum1")]

                issue_weights("M0", 16)
                winb = wslot[0:7]
                woutb = wslot[7:9]

                b_xin = Buf("xin")
                sqb = [Buf("sq0"), Buf("sq1")]; tmpb = [Buf("tmp0"), Buf("tmp1")]
                hb = [Buf("h%d" % c) for c in range(8)]; b_inv = Buf("inv")
                acb = [Buf("ac0"), Buf("ac1")]; zb = [Buf("z%d" % j) for j in range(4)]
                ycb = [Buf("yc%d" % j) for j in range(8)]
                kAb = [Buf("kA%d" % j) for j in range(4)]; kBb = [Buf("kB%d" % j) for j in range(4)]
                qsb = [Buf("qs%d" % j) for j in range(4)]; b_cref = Buf("cref"); kdb = [Buf("kd0")]
                ftbB = [Buf("ftB%d" % j) for j in range(6)]; kdbB = [Buf("kdB0")]; b_crefB = Buf("crefB")
                b_kdT = Buf("kdT"); qeb = [Buf("qe%d" % j) for j in range(4)]
                b_v = Buf("v"); sgb = [Buf("sg%d" % j) for j in range(4)]; b_el = Buf("el")
                scTb = [Buf("scT0"), Buf("scT1")]; osqb = [Buf("osq0"), Buf("osq1")]
                b_invo = Buf("invo"); b_t1 = Buf("t1"); b_S = Buf("S"); Sbb = [Buf("Sb%d" % i) for i in range(3)]
                d_xin = P.dsem(); d_st = P.dsem()
                B_main = [bank[0], bank[1]]; B_x = [bank[2], bank[3]]
                B_norm, B_U, B_v = bank[4], bank[5], bank[6]
                B_sc, B_o = 2, 3

                P.op("dve", lambda e: e.memset(z[:], 0.0), writes=zb)
                P.op("dve", lambda e: e.memset(Sst[:], 0.0), writes=[b_S])
                P.op("dve", lambda e: e.memset(cref2[:], 0.0), writes=[b_cref])
                P.op("dve", lambda e: e.memset(Sb[:, 0, :], 0.0), writes=[Sbb[0]])
                kglob = [0]

                def load_xin(ti):
                    P.op("sp", lambda e: e.dma_start(
                        out=xin[:], in_=x_d[ti * TT:(ti + 1) * TT, :].rearrange("(g p) d -> p g d", p=128)),
                        writes=[b_xin], dsem=d_xin)

                def proj_fm(bk, wslot_ap, wbuf, col0):
                    def f(e):
                        ins = None
                        for kc in range(8):
                            ins = mm(e, ps[:, bk, :], wslot_ap[:, kc, col0:col0 + 128], hT[:, kc, :], kc == 0, kc == 7)
                        return ins
                    P.op("pe", f, reads=[wbuf] + hb, writes=[bank[bk]])

                cut(20)
                load_xin(0)
                nmain = [0]

                def next_main():
                    nmain[0] += 1
                    return nmain[0] % 2

                nx = [0]

                def next_x():
                    nx[0] += 1
                    return 2 + nx[0] % 2

                def A_gen(xTn, xTbn):
                    for c in range(8):
                        bk = c % 2
                        def f(e, c=c, bk=bk):
                            ins = None
                            for g in range(4):
                                ins = e.transpose(out=ps[:, bk, g * 128:(g + 1) * 128],
                                                  in_=xin[:, g, c * 128:(c + 1) * 128], identity=ident)
                            return ins
                        P.op("pe", f, reads=[b_xin, b_consts], writes=[bank[bk]])
                        P.op("dve", lambda e, c=c, bk=bk: e.tensor_copy(out=xTn[:, c, :], in_=ps[:, bk, :]),
                             reads=[bank[bk]], writes=[xTbn[c]])
                        yield
                    norm_and_modulate(xTn, xTbn, sqt, sqb, tmp, tmpb, hT, hb, invB, b_inv, B_norm, 4, l, 0)
                    yield

                for _ in A_gen(Rg[0], RgB[0]):
                    pass
                for ti in range(ntiles):
                    cur = ti % 2
                    xT, xTb = Rg[cur], RgB[cur]
                    ft, ftb = Rg[1 - cur][:, 0:6, :], RgB[1 - cur][0:6]
                    acc, accb = Rg[1 - cur][:, 6:8, :], RgB[1 - cur][6:8]
                    cut(2)
                    def conv_body(j, acc=acc, accb=accb):
                        bk = next_x()
                        proj_fm(bk, win[:, 1], winb[1], j * 128)
                        P.op("act", lambda e, j=j, bk=bk: e.activation(out=ac[:, j % 2, :], in_=ps[:, bk, :], func=AF.Copy),
                             reads=[bank[bk]], writes=[acb[j % 2]])
                        yield
                        bk = next_x()
                        proj_fm(bk, win[:, 2], winb[2], j * 128)
                        P.op("dve", lambda e, j=j, bk=bk: e.tensor_tensor(out=z[:, j, 2:TT + 2], in0=ps[:, bk, :],
                                                                          in1=ac[:, j % 2, :], op=ALU.mult),
                             reads=[bank[bk], acb[j % 2]], writes=[zb[j]])
                        P.op("act", lambda e, j=j: e.activation(out=acc[:, j % 2, :], in_=z[:, j, 2:TT + 2],
                                                                func=AF.Identity, bias=vc(V_CONVB, j),
                                                                scale=vc(V_CONVW, 2 * 4 + j)),
                             reads=[zb[j], b_vcol], writes=[accb[j % 2]])
                        for tap in (1, 0):
                            P.op("dve", lambda e, j=j, tap=tap: e.scalar_tensor_tensor(
                                out=acc[:, j % 2, :], in0=z[:, j, tap:tap + TT], scalar=vc(V_CONVW, tap * 4 + j),
                                in1=acc[:, j % 2, :], op0=ALU.mult, op1=ALU.add),
                                reads=[zb[j], accb[j % 2], b_vcol], writes=[accb[j % 2]])
                        P.op("act", lambda e, j=j: e.activation(out=z[:, j, 0:2], in_=z[:, j, TT:TT + 2], func=AF.Copy),
                             reads=[zb[j]], writes=[zb[j]])
                        yield
                        bk = next_x()
                        proj_fm(bk, win[:, 0], winb[0], j * 128)
                        P.op("dve", lambda e, j=j, bk=bk: e.tensor_tensor(out=ycat[:, j, :], in0=ps[:, bk, :],
                                                                          in1=acc[:, j % 2, :], op=ALU.mult),
                             reads=[bank[bk], accb[j % 2]], writes=[ycb[j]])
                        yield
                    cut(3)
                    def gates_body(j, S_ft, S_ftb, S_kd, S_kdb, S_cref2, S_bcref, S_bf, S_bq):
                        bk = S_bf
                        proj_fm(bk, win[:, 4], winb[4], j * 128)
                        P.op("act", lambda e, bk=bk: e.activation(out=S_ft[:, 0, :], in_=ps[:, bk, :], func=AF.Sigmoid),
                             reads=[bank[bk]], writes=[S_ftb[0]])
                        yield
                        P.op("act", lambda e, j=j: e.activation(out=S_ft[:, 1, :], in_=S_ft[:, 0, :], func=AF.Ln,
                                                                bias=lbc[:, j:j + 1], scale=lbc[:, 4 + j:5 + j]),
                             reads=[S_ftb[0], b_small], writes=[S_ftb[1]])
                        yield
                        P.op("dve", lambda e, j=j: e.tensor_scalar(out=S_ft[:, 2, :], in0=S_ft[:, 0, :],
                                                                   scalar1=lbc[:, 8 + j:9 + j], scalar2=lbc[:, 4 + j:5 + j],
                                                                   op0=ALU.mult, op1=ALU.add),
                             reads=[S_ftb[0], b_small], writes=[S_ftb[2]])
                        yield
                        P.op("dve", lambda e: e.tensor_tensor_scan(out=S_ft[:, 3, :], data0=cmask, data1=S_ft[:, 1, :],
                                                                   initial=0.0, op0=ALU.mult, op1=ALU.add),
                             reads=[S_ftb[1], b_consts], writes=[S_ftb[3]])
                        yield
                        cum3 = S_ft[:, 3, :].rearrange("p (c t) -> p c t", t=64)
                        cum4 = S_ft[:, 3, :].rearrange("p (c h t) -> p c h t", h=2, t=32)
                        P.op("act", lambda e: e.activation(out=S_ft[:, 4, :], in_=S_ft[:, 3, :], func=AF.Exp),
                             reads=[S_ftb[3]], writes=[S_ftb[4]])
                        yield
                        P.op("dve", lambda e, j=j: e.tensor_copy(
                            out=elb[:, j, :], in_=S_ft[:, 4, :].rearrange("p (c t) -> p c t", t=64)[:, :, 63]),
                            reads=[S_ftb[4]], writes=[b_el])
                        yield
                        P.op("dve", lambda e, cum3=cum3: e.tensor_copy(out=S_cref2[:, :, 1], in_=cum3[:, :, 31]),
                             reads=[S_ftb[3]], writes=[S_bcref])
                        yield
                        P.op("dve", lambda e, cum3=cum3: e.tensor_tensor(
                            out=S_ft[:, 0, :].rearrange("p (c t) -> p c t", t=64), in0=cum3,
                            in1=cum3[:, :, 63:64].to_broadcast([128, 8, 64]), op=ALU.subtract),
                            reads=[S_ftb[3]], writes=[S_ftb[0]])
                        yield
                        P.op("act", lambda e: e.activation(out=S_ft[:, 0, :], in_=S_ft[:, 0, :], func=AF.Exp, scale=-1.0),
                             reads=[S_ftb[0]], writes=[S_ftb[0]])
                        yield
                        P.op("dve", lambda e: e.tensor_tensor(out=S_kd[:, 0, :], in0=S_ft[:, 2, :], in1=S_ft[:, 0, :], op=ALU.mult),
                             reads=[S_ftb[2], S_ftb[0]], writes=[S_kdb[0]])
                        yield
                        P.op("dve", lambda e, cum4=cum4: e.tensor_tensor(
                            out=S_ft[:, 1, :].rearrange("p (c h t) -> p c h t", h=2, t=32), in0=cum4,
                            in1=S_cref2[:].unsqueeze(3).to_broadcast([128, 8, 2, 32]), op=ALU.subtract),
                            reads=[S_ftb[3], S_bcref], writes=[S_ftb[1]])
                        yield
                        P.op("act", lambda e: e.activation(out=S_ft[:, 1, :], in_=S_ft[:, 1, :], func=AF.Exp),
                             reads=[S_ftb[1]], writes=[S_ftb[1]])
                        yield
                        P.op("dve", lambda e: e.tensor_scalar(out=S_ft[:, 5, :], in0=S_ft[:, 3, :], scalar1=-85.0, scalar2=None,
                                                              op0=ALU.max), reads=[S_ftb[3]], writes=[S_ftb[5]])
                        yield
                        P.op("act", lambda e: e.activation(out=S_ft[:, 5, :], in_=S_ft[:, 5, :], func=AF.Exp, scale=-1.0),
                             reads=[S_ftb[5]], writes=[S_ftb[5]])
                        yield
                        P.op("dve", lambda e, j=j: e.tensor_tensor(out=kA[:, j, :], in0=S_ft[:, 2, :], in1=S_ft[:, 5, :],
                                                                   op=ALU.mult),
                             reads=[S_ftb[2], S_ftb[5]], writes=[kAb[j]])
                        yield
                        P.op("dve", lambda e, cum3=cum3: e.tensor_tensor(
                            out=S_ft[:, 0, :].rearrange("p (c t) -> p c t", t=64), in0=cum3,
                            in1=S_cref2[:, :, 1:2].to_broadcast([128, 8, 64]), op=ALU.subtract),
                            reads=[S_ftb[3], S_bcref], writes=[S_ftb[0]])
                        yield
                        P.op("dve", lambda e: e.tensor_scalar(out=S_ft[:, 0, :], in0=S_ft[:, 0, :], scalar1=-85.0, scalar2=None,
                                                              op0=ALU.max), reads=[S_ftb[0]], writes=[S_ftb[0]])
                        yield
                        P.op("act", lambda e: e.activation(out=S_ft[:, 0, :], in_=S_ft[:, 0, :], func=AF.Exp, scale=-1.0),
                             reads=[S_ftb[0]], writes=[S_ftb[0]])
                        yield
                        P.op("dve", lambda e, j=j: e.tensor_tensor(out=kB[:, j, :], in0=S_ft[:, 2, :], in1=S_ft[:, 0, :],
                                                                   op=ALU.mult),
                             reads=[S_ftb[2], S_ftb[0]], writes=[kBb[j]])
                        yield
                        bk = S_bq
                        proj_fm(bk, win[:, 3], winb[3], j * 128)
                        P.op("dve", lambda e, j=j, bk=bk: e.tensor_tensor(out=qe[:, j, :], in0=ps[:, bk, :],
                                                                          in1=S_ft[:, 4, :], op=ALU.mult),
                             reads=[bank[bk], S_ftb[4]], writes=[qeb[j]])
                        yield
                        P.op("dve", lambda e, j=j, bk=bk: e.tensor_tensor(out=qs[:, j, :], in0=ps[:, bk, :],
                                                                          in1=S_ft[:, 1, :], op=ALU.mult),
                             reads=[bank[bk], S_ftb[1]], writes=[qsb[j]])
                        yield
                        def f(e, j=j):
                            ins = None
                            for g in range(4):
                                ins = e.transpose(out=psb[:, g * 128:(g + 1) * 128],
                                                  in_=S_kd[:, 0, g * 128:(g + 1) * 128], identity=identb[:])
                            return ins
                        P.op("pe", f, reads=[S_kdb[0], b_small], writes=[bankb])
                        P.op("act", lambda e, j=j: e.activation(
                            out=kdT[:, :, j * 128:(j + 1) * 128],
                            in_=psb[:, 0:512].rearrange("p (g k) -> p g k", k=128), func=AF.Copy),
                            reads=[bankb], writes=[b_kdT])
                        yield
                    def bg_body(j):
                        bk = next_x()
                        proj_fm(bk, win[:, 6], winb[6], j * 128)
                        P.op("act", lambda e, j=j, bk=bk: e.activation(out=sg[:, j, :], in_=ps[:, bk, :], func=AF.Silu),
                             reads=[bank[bk]], writes=[sgb[j]])
                        yield
                    def bi_body(g):
                        def f(e, g=g):
                            ins = None
                            for kc in range(8):
                                ins = mm(e, ps[:, 6, :], hT[:, kc, g * 128:(g + 1) * 128], win[:, 5, kc, :], kc == 0, kc == 7)
                            return ins
                        P.op("pe", f, reads=[winb[5]] + hb, writes=[B_v])
                        P.op("act", lambda e, g=g: e.activation(out=vt[:, g, :], in_=ps[:, 6, :], func=AF.Copy),
                             reads=[B_v], writes=[b_v])
                        yield

                    def chain_gens(gens):
                        for g_ in gens:
                            yield from g_
                    def bg_all():
                        for j in range(4):
                            for _ in bg_body(j):
                                pass
                        yield
                    gx = chain_gens([conv_body(j) for j in range(4)] + [bg_all()]
                                    + [bi_body(g) for g in range(4)])
                    setA = (ft, ftb, kd, kdb, cref2, b_cref, 0, 1)
                    setB = (ftB, ftbB, kdB, kdbB, cref2B, b_crefB, 4, 5)

                    def rr2(ga, gb):
                        la = lb = True
                        while la or lb:
                            if la:
                                try:
                                    next(ga)
                                except StopIteration:
                                    la = False
                            if lb:
                                try:
                                    next(gb)
                                except StopIteration:
                                    lb = False
                            yield
                    P.op("dve", lambda e: e.memset(cref2B[:], 0.0), writes=[b_xin, b_crefB, kdbB[0]] + ftbB)
                    gy = chain_gens([rr2(gates_body(0, *setA), gates_body(1, *setB)),
                                     rr2(gates_body(2, *setA), gates_body(3, *setB))])
                    alive_x = alive_y = True
                    while alive_x or alive_y:
                        for _ in range(2):
                            if alive_y:
                                try:
                                    next(gy)
                                except StopIteration:
                                    alive_y = False
                        if alive_x:
                            try:
                                next(gx)
                            except StopIteration:
                                alive_x = False
                    P.op("dve", lambda e: e.memset(cref2B[:, 0, 0:1], 0.0), writes=[b_xin, b_crefB, kdbB[0]] + ftbB)
                    if ti + 1 < ntiles:
                        load_xin(ti + 1)
                    cut(4)
                    def z1(g):
                        def f(e, g=g):
                            ins = None
                            for j in range(4):
                                for I in range(4):
                                    src = kA if I % 2 == 0 else kB
                                    ins = mm(e, ps[:, B_sc, j * 128 + I * 32:j * 128 + I * 32 + 32],
                                             src[:, j, g * 128:(g + 1) * 128],
                                             qs[:, j, g * 128 + I * 32:g * 128 + I * 32 + 32], True, True)
                            return ins
                        P.op("pe", f, reads=kAb + kBb + qsb, writes=[bank[B_sc]])
                        P.op("dve", lambda e, g=g: e.tensor_tensor(
                            out=scT[:, g % 2], in0=ps[:, B_sc, :].rearrange("p (j t) -> p j t", t=128),
                            in1=maskbd[:].unsqueeze(1).to_broadcast([128, 4, 128]), op=ALU.mult),
                            reads=[bank[B_sc], b_small], writes=[scTb[g % 2]])

                        k0 = kglob[0]

                        def do_U(half, g=g, k0=k0):
                            nxt = (k0 + half + 1) % 3
                            def f(e, half=half, g=g):
                                ins = None
                                r0 = half * 64
                                for j in range(4):
                                    ins = mm(e, ps[:, 5, j * 128:(j + 1) * 128], kdT[r0:r0 + 64, g, j * 128:(j + 1) * 128],
                                             vt[r0:r0 + 64, g, j * 128:(j + 1) * 128], True, True)
                                return ins
                            P.op("pe", f, reads=[b_kdT, b_v], writes=[B_U])
                            ci = g * 2 + half
                            for j in range(4):
                                P.op("dve", lambda e, j=j, ci=ci: e.scalar_tensor_tensor(
                                    out=Sst[:, j * 128:(j + 1) * 128], in0=Sst[:, j * 128:(j + 1) * 128],
                                    scalar=elb[:, j, ci:ci + 1], in1=ps[:, 5, j * 128:(j + 1) * 128],
                                    op0=ALU.mult, op1=ALU.add), reads=[b_S, b_el, B_U], writes=[b_S])
                            P.op("act", lambda e, nxt=nxt: e.activation(out=Sb[:, nxt, :], in_=Sst[:], func=AF.Copy),
                                 reads=[b_S], writes=[Sbb[nxt]])

                        do_U(0)

                        def f(e, g=g):
                            ins = None
                            for j in range(4):
                                ins = mm(e, ps[:, B_o, j * 128:(j + 1) * 128], vt[:, g, j * 128:(j + 1) * 128],
                                         scT[:, g % 2, j, :], True, True)
                            return ins
                        P.op("pe", f, reads=[b_v, scTb[g % 2]], writes=[bank[B_o]])

                        def f(e, g=g, k0=k0):
                            ins = None
                            for j in range(4):
                                for half in range(2):
                                    cur = (k0 + half) % 3
                                    t0 = g * 128 + half * 64
                                    ins = mm(e, ps[:, 6, j * 128 + half * 64:j * 128 + half * 64 + 64],
                                             Sb[:, cur, j * 128:(j + 1) * 128], qe[:, j, t0:t0 + 64], True, True)
                            return ins
                        P.op("pe", f, reads=[Sbb[k0 % 3], Sbb[(k0 + 1) % 3]] + qeb, writes=[B_v])
                        P.op("act", lambda e: e.activation(out=oi[:], in_=ps[:, 6, :], func=AF.Copy),
                             reads=[B_v], writes=[b_oi])
                        P.op("dve", lambda e, g=g: e.tensor_tensor(out=osum[:, g % 2, :], in0=ps[:, B_o, :], in1=oi[:], op=ALU.add),
                             reads=[bank[B_o], b_oi], writes=[b_osum[g % 2]])
                        do_U(1)
                        kglob[0] += 2

                    def z2(g):
                        P.op("act", lambda e, g=g: e.activation(out=osq[:, g % 2, :], in_=osum[:, g % 2, :], func=AF.Square),
                             reads=[b_osum[g % 2]], writes=[osqb[g % 2]])
                        P.op("pe", lambda e, g=g: mm(e, ps[:, 4, :], ones128[:], osq[:, g % 2, :], True, True),
                             reads=[osqb[g % 2], b_small], writes=[B_norm])
                        rsqrt_eps(invo[:], b_invo, ps[:, 4, :], B_norm)
                        P.op("dve", lambda e, g=g: e.tensor_tensor(out=t1[:], in0=osum[:, g % 2, :], in1=invo[:], op=ALU.mult),
                             reads=[b_osum[g % 2], b_invo], writes=[b_t1])
                        for j in range(4):
                            P.op("dve", lambda e, j=j, g=g: e.scalar_tensor_tensor(
                                out=ycat[:, 4 + j, g * 128:(g + 1) * 128], in0=t1[:, j * 128:(j + 1) * 128],
                                scalar=vc(V_GAIN, j), in1=sg[:, j, g * 128:(g + 1) * 128], op0=ALU.mult, op1=ALU.mult),
                                reads=[b_t1, sgb[j], b_vcol], writes=[ycb[4 + j]])

                    if ti + 1 < ntiles:
                        ag = A_gen(Rg[1 - cur], RgB[1 - cur])
                    else:
                        ag = iter(())

                    def a_steps(n):
                        for _ in range(n):
                            try:
                                next(ag)
                            except StopIteration:
                                return
                    z1(0); a_steps(2); z1(1); a_steps(2); z2(0); a_steps(2); z1(2); a_steps(2); z2(1); a_steps(1)
                    z1(3); z2(2); z2(3); a_steps(8)
                    cut(5)
                    for c in range(8):
                        bk = next_main()
                        def f(e, c=c, bk=bk):
                            ins = None
                            for k in range(8):
                                ins = mm(e, ps[:, bk, :], wout[:, c // 4, k, (c % 4) * 128:(c % 4) * 128 + 128],
                                         ycat[:, k, :], k == 0, k == 7)
                            return ins
                        P.op("pe", f, reads=[woutb[c // 4]] + ycb, writes=[bank[bk]])
                        P.op("dve", lambda e, c=c, bk=bk, xT=xT: e.scalar_tensor_tensor(
                            out=xT[:, c, :], in0=ps[:, bk, :], scalar=mod(l, 2, c), in1=xT[:, c, :],
                            op0=ALU.mult, op1=ALU.add), reads=[bank[bk], xTb[c], b_small], writes=[xTb[c]])
                    store_xs(xT, xTb, ti, d_st)
                if next_ph is not None:
                    issue_weights(next_ph, 9)
                P.barrier(exclude=wdsem)

        def phase_ffn(l, final, next_ph=None):
            with contextlib.ExitStack() as st:
                areset()
                a = aalloc
                pre = "f%d_" % l
                w1 = a(pre + "w1", [128, 8, 8, 512], BF16)
                w2 = a(pre + "w2", [128, 8, 4, D], BF16)
                xT2 = a(pre + "xT", [128, 2, 8, TT], F32)
                sqt = a(pre + "sq", [128, 2, TT], BF16)
                tmp = a(pre + "tmp", [128, 1 if final else 2, TT], F32)
                hT = a(pre + "h", [128, 8, TT], BF16)
                invB = a(pre + "inv", [128, TT], F32)
                hid = a(pre + "hid", [128, 16, TT], BF16)
                rt = a(pre + "rt", [128, 2, TT], BF16)
                yout = a(pre + "yout", [128, 1, D], F32) if final else None
                if final:
                    fgB = a(pre + "fgB", [128, D], F32)
                    fst = a(pre + "fst", [128, 16], F32)
                    b_fg = Buf("fgB"); b_fst = Buf("fst")
                    d_fg = P.dsem()
                    P.op("sp", lambda e: e.dma_start(
                        out=fgB[:], in_=rows_d[0:1, R_FG:R_FG + D].partition_broadcast(128)),
                        writes=[b_fg], dsem=d_fg)

                issue_weights("F%d" % l, 16)
                w1b = wslot[0:8]
                w2b = wslot[8:16]

                xTb2 = [[Buf("xT%d_%d" % (s_, c)) for c in range(8)] for s_ in range(2)]
                sqb = [Buf("sq0"), Buf("sq1")]; tmpb = [Buf("tmp0"), Buf("tmp1")]
                hb = [Buf("h%d" % c) for c in range(8)]; b_inv = Buf("inv")
                hidb = [Buf("hid%d" % i) for i in range(16)]; rtb = [Buf("rt0"), Buf("rt1")]
                youtb = [Buf("yout0")]
                d_ld = [P.dsem(), P.dsem()]; d_st = [P.dsem(), P.dsem()]; d_out = [P.dsem()]
                nmain = [0]

                def next_main():
                    nmain[0] += 1
                    return nmain[0] % 4

                def ff1(half):
                    for fcl in range(16):
                        fc = half * 16 + fcl
                        bk = next_main()
                        def f(e, fc=fc, bk=bk):
                            ins = None
                            for kc in range(8):
                                ins = mm(e, ps[:, bk, :], w1[:, fc // 4, kc, (fc % 4) * 128:(fc % 4) * 128 + 128],
                                         hT[:, kc, :], kc == 0, kc == 7)
                            return ins
                        P.op("pe", f, reads=[w1b[fc // 4]] + hb, writes=[bank[bk]])
                        P.op("act", lambda e, fc=fc, bk=bk: e.activation(out=rt[:, fc % 2, :], in_=ps[:, bk, :], func=AF.Relu),
                             reads=[bank[bk]], writes=[rtb[fc % 2]])
                        P.op("dve", lambda e, fc=fc, fcl=fcl: e.tensor_tensor(out=hid[:, fcl, :], in0=rt[:, fc % 2, :],
                                                                              in1=rt[:, fc % 2, :], op=ALU.mult),
                             reads=[rtb[fc % 2]], writes=[hidb[fcl]])
                        yield

                def ff2(half, xT, xTb):
                    for c in range(8):
                        bk = next_main()
                        def f(e, c=c, bk=bk):
                            ins = None
                            for fcl in range(16):
                                fc = half * 16 + fcl
                                ins = mm(e, ps[:, bk, :], w2[:, fc // 4, fc % 4, c * 128:(c + 1) * 128], hid[:, fcl, :],
                                         fcl == 0, fcl == 15)
                            return ins
                        P.op("pe", f, reads=w2b[half * 4:(half + 1) * 4] + hidb, writes=[bank[bk]])
                        P.op("dve", lambda e, c=c, bk=bk: e.scalar_tensor_tensor(
                            out=xT[:, c, :], in0=ps[:, bk, :], scalar=mod(l, 5, c), in1=xT[:, c, :],
                            op0=ALU.mult, op1=ALU.add), reads=[bank[bk], xTb[c], b_small], writes=[xTb[c]])

                load_xs(xT2[:, 0], xTb2[0], 0, d_ld[0])
                norm_and_modulate(xT2[:, 0], xTb2[0], sqt, sqb, tmp, tmpb, hT, hb, invB, b_inv, bank[4], 4, l, 1)
                def final_gen(ti, xT, xTb):
                    for g in range(4):
                        for hh in range(2):
                            bk = 5 + hh
                            def f(e, g=g, hh=hh, bk=bk, xT=xT):
                                ins = None
                                for cc in range(4):
                                    c = hh * 4 + cc
                                    ins = e.transpose(out=ps[:, bk, cc * 128:(cc + 1) * 128],
                                                      in_=xT[:, c, g * 128:(g + 1) * 128], identity=ident)
                                return ins
                            P.op("pe", f, reads=xTb + [b_consts], writes=[bank[bk]])
                        if do_final_norm:
                            for hh in range(2):
                                P.op("dve", lambda e, hh=hh: e.bn_stats(out=fst[:, hh * 6:hh * 6 + 6], in_=ps[:, 5 + hh, :]),
                                     reads=[bank[5 + hh]], writes=[b_fst])
                            P.op("dve", lambda e: e.bn_aggr(out=fst[:, 12:14], in_=fst[:, 0:12]), reads=[b_fst], writes=[b_fst])
                            P.op("dve", lambda e: e.scalar_tensor_tensor(out=fst[:, 14:15], in0=fst[:, 12:13],
                                                                         scalar=fst[:, 12:13], in1=fst[:, 13:14],
                                                                         op0=ALU.mult, op1=ALU.add),
                                 reads=[b_fst], writes=[b_fst])
                            rsqrt_eps(fst[:, 15:16], b_fst, fst[:, 14:15], b_fst)
                            for hh in range(2):
                                P.op("dve", lambda e, hh=hh: e.scalar_tensor_tensor(
                                    out=yout[:, 0, hh * 512:(hh + 1) * 512], in0=ps[:, 5 + hh, :], scalar=fst[:, 15:16],
                                    in1=fgB[:, hh * 512:(hh + 1) * 512], op0=ALU.mult, op1=ALU.mult),
                                    reads=[bank[5 + hh], b_fst, b_fg], writes=[youtb[0]])
                        else:
                            for hh in range(2):
                                P.op("dve", lambda e, hh=hh: e.tensor_copy(out=yout[:, 0, hh * 512:(hh + 1) * 512],
                                                                           in_=ps[:, 5 + hh, :]),
                                     reads=[bank[5 + hh]], writes=[youtb[0]])
                        r0 = ti * TT + g * 128
                        P.op("sp", lambda e, g=g, r0=r0: e.dma_start(out=out_d[r0:r0 + 128, :], in_=yout[:, 0, :]),
                             reads=[youtb[0]], dsem=d_out[0])
                        yield

                pending = None
                for ti in range(ntiles):
                    cur = ti % 2
                    xT, xTb = xT2[:, cur], xTb2[cur]
                    if not final and ti + 1 < ntiles:
                        load_xs(xT2[:, 1 - cur], xTb2[1 - cur], ti + 1, d_ld[1 - cur])
                    k = 0
                    for _ in ff1(0):
                        k += 1
                        if pending is not None and k % 4 == 0:
                            try:
                                next(pending)
                            except StopIteration:
                                pending = None
                    if pending is not None:
                        for _ in pending:
                            pass
                        pending = None
                    if final and ti + 1 < ntiles:
                        load_xs(xT2[:, 1 - cur], xTb2[1 - cur], ti + 1, d_ld[1 - cur])
                    ff2(0, xT, xTb)
                    for _ in ff1(1):
                        pass
                    if ti + 1 < ntiles:
                        norm_and_modulate(xT2[:, 1 - cur], xTb2[1 - cur], sqt, sqb, tmp, tmpb, hT, hb, invB, b_inv,
                                          bank[4], 4, l, 1)
                    ff2(1, xT, xTb)
                    if not final:
                        store_xs(xT, xTb, ti, d_st[cur])
                        continue
                    pending = final_gen(ti, xT, xTb)
                if pending is not None:
                    for _ in pending:
                        pass
                if next_ph is not None:
                    issue_weights(next_ph, 16)
                P.barrier(exclude=wdsem)

        def phase_m1(next_ph=None):
            l = 1
            with contextlib.ExitStack() as st:
                areset()
                a = aalloc
                win = a("m1_win", [128, 4, 8, 512], BF16)
                wout = a("m1_wout", [128, 2, 8, 512], BF16)
                xT2 = a("m1_xT", [128, 2, 8, TT], F32)
                sqt = a("m1_sq", [128, 2, TT], BF16)
                tmp = a("m1_tmp", [128, 2, TT], F32)
                hT = a("m1_h", [128, 8, TT], BF16)
                invB = a("m1_inv", [128, TT], F32)
                uT = a("m1_u", [128, 8, TT], BF16)
                gv = a("m1_gv", [128, 2, D], F32)
                vn = a("m1_vn", [128, 2, D], BF16)
                st6 = a("m1_st6", [128, 2, 2, 6], F32)
                mv = a("m1_mv", [128, 2, 4], F32)
                t1 = a("m1_t1", [128, 2, 512], F32)
                ymix = a("m1_ymix", [128, 8, TT], BF16)
                RB = a("m1_RB", [128, 8, 128], F32)
                ws32 = a("m1_ws32", [128, 4, 128], F32)
                wmT32 = a("m1_wmT32", [128, 4, 128], F32)
                wmT = a("m1_wmT", [128, 4, 128], BF16)
                onesbb = a("m1_onesbb", [128, 128], BF16)
                bvB = a("m1_bvB", [128, D], F32)
                bsB = a("m1_bsB", [128, 512], F32)
                gpre = a("m1_gpre", [128, 2, 2, 512], F32)
                gpreb = [[Buf("gpre%d%d" % (i_, h_)) for h_ in range(2)] for i_ in range(2)]
                b_stg = [Buf("stg0"), Buf("stg1")]
                xTb2 = [[Buf("xT%d_%d" % (s_, c)) for c in range(8)] for s_ in range(2)]
                d_ld2 = [P.dsem(), P.dsem()]; d_st2 = [P.dsem(), P.dsem()]
                d_r = P.dsem()
                P.op("sp", lambda e: e.dma_start(out=bvB[:], in_=rows_d[0:1, R_BV:R_BV + D].partition_broadcast(128)),
                     writes=[b_rows], dsem=d_r)
                P.op("sp", lambda e: e.dma_start(out=bsB[:], in_=rows_d[0:1, R_BS:R_BS + 512].partition_broadcast(128)),
                     writes=[b_rows], dsem=d_r)

                issue_weights("M1", 16)
                winb = wslot[0:4]
                woutb = wslot[4:6]

                xTb = [Buf("xT%d" % c) for c in range(8)]
                sqb = [Buf("sq0"), Buf("sq1")]; tmpb = [Buf("tmp0"), Buf("tmp1")]
                hb = [Buf("h%d" % c) for c in range(8)]; b_inv = Buf("inv")
                ub = [Buf("u%d" % c) for c in range(8)]
                gvb = [Buf("gv0"), Buf("gv1")]; vnb = [Buf("vn0"), Buf("vn1")]
                b_st = Buf("st6"); t1b = [Buf("t10"), Buf("t11")]
                ymb = [Buf("ym%d" % c) for c in range(8)]
                b_m1c = Buf("m1consts"); b_ws = Buf("ws32")
                d_ld = P.dsem(); d_st = P.dsem(); d_ws = P.dsem()

                P.op("sp", lambda e: e.dma_start(out=ws32[:], in_=gmws_d.rearrange("g t s -> t g s")),
                     writes=[b_ws], dsem=d_ws)
                P.op("dve", lambda e: e.tensor_tensor(out=ws32[:], in0=ws32[:],
                                                      in1=tril.unsqueeze(1).to_broadcast([128, 4, 128]), op=ALU.mult),
                     reads=[b_ws, b_consts], writes=[b_ws])
                def f(e):
                    ins = None
                    for g in range(4):
                        ins = e.transpose(out=ps[:, 0, g * 128:(g + 1) * 128], in_=ws32[:, g, :], identity=ident)
                    return ins
                P.op("pe", f, reads=[b_ws, b_consts], writes=[bank[0]])
                P.op("dve", lambda e: e.tensor_copy(out=wmT32[:], in_=ps[:, 0, :].rearrange("p (g t) -> p g t", t=128)),
                     reads=[bank[0]], writes=[b_m1c])
                P.op("act", lambda e: e.activation(out=wmT[:], in_=ps[:, 0, :].rearrange("p (g t) -> p g t", t=128),
                                                   func=AF.Copy), reads=[bank[0]], writes=[b_m1c])
                P.op("dve", lambda e: e.memset(onesbb[:], 1.0), writes=[b_m1c])
                P.op("pe", lambda e: mm(e, ps[:, 1, :], onesbb[:], wmT[:].rearrange("p g t -> p (g t)"), True, True),
                     reads=[b_m1c], writes=[bank[1]])
                for ec in range(8):
                    g = ec // 2
                    P.op("dve", lambda e, ec=ec, g=g: e.scalar_tensor_tensor(
                        out=RB[:, ec, :], in0=ps[:, 1, g * 128:(g + 1) * 128], scalar=vc(V_LNB, ec),
                        in1=bsB[:, g * 128:(g + 1) * 128], op0=ALU.mult, op1=ALU.add),
                        reads=[bank[1], b_vcol, b_rows], writes=[b_m1c])

                nmain = [0]

                def next_main():
                    nmain[0] += 1
                    return nmain[0] % 2

                def u_proj(ec):
                    bk = next_main()
                    def f(e, ec=ec, bk=bk):
                        ins = None
                        for kc in range(8):
                            ins = mm(e, ps[:, bk, :], win[:, ec // 4, kc, (ec % 4) * 128:(ec % 4) * 128 + 128],
                                     hT[:, kc, :], kc == 0, kc == 7)
                        return ins
                    P.op("pe", f, reads=[winb[ec // 4]] + hb, writes=[bank[bk]])
                    P.op("act", lambda e, ec=ec, bk=bk: e.activation(out=uT[:, ec, :], in_=ps[:, bk, :],
                                                                     func=AF.Gelu_apprx_tanh, bias=vc(V_BIN1U, ec),
                                                                     scale=1.0),
                         reads=[bank[bk], b_vcol], writes=[ub[ec]])

                def stage_a(g):
                    for hh in range(2):
                        bk = 2 + hh
                        def f(e, g=g, hh=hh, bk=bk):
                            ins = None
                            for kc in range(8):
                                ins = mm(e, ps[:, bk, :], hT[:, kc, g * 128:(g + 1) * 128], win[:, 2 + hh, kc, :], kc == 0, kc == 7)
                            return ins
                        P.op("pe", f, reads=[winb[2 + hh]] + hb, writes=[bank[bk]])
                        P.op("dve", lambda e, g=g, hh=hh, bk=bk: e.tensor_tensor(
                            out=gpre[:, g % 2, hh, :], in0=ps[:, bk, :], in1=bvB[:, hh * 512:(hh + 1) * 512], op=ALU.add),
                            reads=[bank[bk], b_rows], writes=[gpreb[g % 2][hh]])
                        P.op("act", lambda e, g=g, hh=hh: e.activation(
                            out=gv[:, g % 2, hh * 512:(hh + 1) * 512], in_=gpre[:, g % 2, hh, :], func=AF.Gelu_apprx_tanh),
                            reads=[gpreb[g % 2][hh]], writes=[gvb[g % 2]])

                def stage_b(g):
                    s6, mvg, bst = st6[:, g % 2], mv[:, g % 2], b_stg[g % 2]
                    for hh in range(2):
                        P.op("dve", lambda e, g=g, hh=hh, s6=s6: e.bn_stats(out=s6[:, hh, :], in_=gv[:, g % 2, hh * 512:(hh + 1) * 512]),
                             reads=[gvb[g % 2]], writes=[bst])
                    P.op("dve", lambda e, s6=s6, mvg=mvg: e.bn_aggr(out=mvg[:, 0:2], in_=s6.rearrange("p a b -> p (a b)")),
                         reads=[bst], writes=[bst])
                    rsqrt_eps(mvg[:, 2:3], bst, mvg[:, 1:2], bst)
                    P.op("dve", lambda e, mvg=mvg: e.scalar_tensor_tensor(out=mvg[:, 3:4], in0=mvg[:, 0:1], scalar=-1.0,
                                                                          in1=mvg[:, 2:3], op0=ALU.mult, op1=ALU.mult),
                         reads=[bst], writes=[bst])
                    P.op("act", lambda e, g=g, mvg=mvg: e.activation(out=vn[:, g % 2, :], in_=gv[:, g % 2, :], func=AF.Identity,
                                                                     bias=mvg[:, 3:4], scale=mvg[:, 2:3]),
                         reads=[gvb[g % 2], bst], writes=[vnb[g % 2]])

                def stage_c(g):
                    for hh in range(2):
                        bk = 5 + hh
                        def f(e, g=g, hh=hh, bk=bk):
                            ins = None
                            for cc in range(4):
                                ec = hh * 4 + cc
                                ins = mm(e, ps[:, bk, cc * 128:(cc + 1) * 128], vn[:, g % 2, ec * 128:(ec + 1) * 128],
                                         wmT[:, ec // 2, :], True, True)
                            return ins
                        P.op("pe", f, reads=[vnb[g % 2], b_m1c], writes=[bank[bk]])
                        for cc in range(4):
                            ec = hh * 4 + cc
                            P.op("dve", lambda e, cc=cc, ec=ec, hh=hh, bk=bk: e.scalar_tensor_tensor(
                                out=t1[:, hh, cc * 128:(cc + 1) * 128], in0=ps[:, bk, cc * 128:(cc + 1) * 128],
                                scalar=vc(V_LNG, ec), in1=RB[:, ec, :], op0=ALU.mult, op1=ALU.add),
                                reads=[bank[bk], b_vcol, b_m1c], writes=[t1b[hh]])
                        P.op("dve", lambda e, g=g, hh=hh: e.tensor_tensor(
                            out=ymix[:, hh * 4:(hh + 1) * 4, g * 128:(g + 1) * 128],
                            in0=t1[:, hh, :].rearrange("p (c t) -> p c t", t=128),
                            in1=uT[:, hh * 4:(hh + 1) * 4, g * 128:(g + 1) * 128], op=ALU.mult),
                            reads=[t1b[hh]] + ub[hh * 4:(hh + 1) * 4], writes=ymb[hh * 4:(hh + 1) * 4])

                cut(31)
                load_xs(xT2[:, 0], xTb2[0], 0, d_ld2[0])
                norm_and_modulate(xT2[:, 0], xTb2[0], sqt, sqb, tmp, tmpb, hT, hb, invB, b_inv, bank[4], 4, l, 0)
                for ti in range(ntiles):
                    cur = ti % 2
                    xT, xTb = xT2[:, cur], xTb2[cur]
                    if ti + 1 < ntiles:
                        load_xs(xT2[:, 1 - cur], xTb2[1 - cur], ti + 1, d_ld2[1 - cur])
                    u_proj(0); u_proj(1); stage_a(0)
                    u_proj(2); u_proj(3); stage_a(1)
                    u_proj(4); u_proj(5); stage_b(0)
                    u_proj(6); u_proj(7); stage_a(2)
                    stage_b(1); stage_c(0); stage_a(3); stage_b(2); stage_c(1); stage_b(3); stage_c(2); stage_c(3)
                    if ti + 1 < ntiles:
                        norm_and_modulate(xT2[:, 1 - cur], xTb2[1 - cur], sqt, sqb, tmp, tmpb, hT, hb, invB, b_inv,
                                          bank[4], 4, l, 0)
                    for c in range(8):
                        bk = next_main()
                        def f(e, c=c, bk=bk):
                            ins = None
                            for k in range(8):
                                ins = mm(e, ps[:, bk, :], wout[:, c // 4, k, (c % 4) * 128:(c % 4) * 128 + 128],
                                         ymix[:, k, :], k == 0, k == 7)
                            return ins
                        P.op("pe", f, reads=[woutb[c // 4]] + ymb, writes=[bank[bk]])
                        P.op("dve", lambda e, c=c, bk=bk, xT=xT: e.scalar_tensor_tensor(
                            out=xT[:, c, :], in0=ps[:, bk, :], scalar=mod(l, 2, c), in1=xT[:, c, :],
                            op0=ALU.mult, op1=ALU.add), reads=[bank[bk], xTb[c], b_small], writes=[xTb[c]])
                    store_xs(xT, xTb, ti, d_st2[cur])
                if next_ph is not None:
                    issue_weights(next_ph, 6)
                P.barrier(exclude=wdsem)

        for pi, ph in enumerate(phases if CUT != 1 else []):
          nxt = phases[pi + 1] if pi + 1 < len(phases) else None
          try:
            if ph == "M0":
                phase_m0(nxt)
            elif ph == "F0":
                phase_ffn(0, final=(phases[-1] == "F0"), next_ph=nxt)
            elif ph == "M1":
                phase_m1(nxt)
            elif ph == "F1":
                cut(35)
                phase_ffn(1, final=True)
          except StopBuild:
            phases = ["F0"]
            break
        if phases[-1] in ("M0", "M1"):
            phase_dump = True
        else:
            phase_dump = False
        if phase_dump:
            if True:
                areset()
                dx = aalloc("dump_x", [128, 8, TT], F32)
                dy = aalloc("dump_y", [128, 2, D], F32)
                dxb = [Buf("dx")]
                dyb = [Buf("dy0"), Buf("dy1")]
                d1 = P.dsem(); d2 = [P.dsem(), P.dsem()]
                for ti in range(ntiles):
                    load_xs(dx, dxb, ti, d1)
                    for g in range(4):
                        for hh in range(2):
                            bk = 5 + hh
                            def f(e, g=g, hh=hh, bk=bk):
                                ins = None
                                for cc in range(4):
                                    c = hh * 4 + cc
                                    ins = e.transpose(out=ps[:, bk, cc * 128:(cc + 1) * 128],
                                                      in_=dx[:, c, g * 128:(g + 1) * 128], identity=ident)
                                return ins
                            P.op("pe", f, reads=dxb + [b_consts], writes=[bank[bk]])
                            P.op("act", lambda e, g=g, hh=hh, bk=bk: e.activation(
                                out=dy[:, g % 2, hh * 512:(hh + 1) * 512], in_=ps[:, bk, :], func=AF.Copy),
                                reads=[bank[bk]], writes=[dyb[g % 2]])
                        r0 = ti * TT + g * 128
                        P.op("sp", lambda e, g=g, r0=r0: e.dma_start(out=out_d[r0:r0 + 128, :], in_=dy[:, g % 2, :]),
                             reads=[dyb[g % 2]], dsem=d2[g % 2])
                P.barrier()
        P.barrier()

        with nc.Block() as block:
            @block.tensor
            def _(e):
                P.emit("pe", e)

            @block.scalar
            def _(e):
                P.emit("act", e)

            @block.vector
            def _(e):
                P.emit("dve", e)

            @block.gpsimd
            def _(e):
                P.emit("pool", e)

            @block.sync
            def _(e):
                P.emit("sp", e)
    return nc


def make_consts():
    c = np.zeros((128, NCONST), np.float32)
    c[:, C_ID:C_ID + 128] = np.eye(128, dtype=np.float32)
    s = np.arange(128)[:, None]
    t = np.arange(128)[None, :]
    c[:, C_MBD:C_MBD + 128] = ((s // 64 == t // 64) & (s <= t)).astype(np.float32)
    c[:, C_TRIL:C_TRIL + 128] = (t <= s).astype(np.float32)
    cm = np.ones((512,), np.float32)
    cm[::64] = 0.0
    c[:, C_CMASK:C_CMASK + 512] = cm[None, :]
    return c


def make_vecs(b, c, ada_b, norm_mix_g, norm_ffn_g, conv_w, conv_b, hg_lb, hg_gain, b_in1, final_g, gm_ln_g, gm_ln_b):
    v = np.zeros((NVEC, 128), np.float32)
    v[V_C:V_C + 8] = c[b].reshape(8, 128)
    v[V_ADAB:V_ADAB + 48] = ada_b[0].reshape(48, 128)
    v[V_ADAB + 48:V_ADAB + 96] = ada_b[1].reshape(48, 128)
    v[V_NMG:V_NMG + 16] = norm_mix_g.reshape(16, 128)
    v[V_NFG:V_NFG + 16] = norm_ffn_g.reshape(16, 128)
    v[V_CONVW:V_CONVW + 12] = conv_w[0].reshape(12, 128)
    v[V_CONVB:V_CONVB + 4] = conv_b[0].reshape(4, 128)
    v[V_HGLB:V_HGLB + 12] = hg_lb.reshape(12, 128)
    v[V_GAIN:V_GAIN + 4] = hg_gain[0].reshape(4, 128)
    v[V_BIN1U:V_BIN1U + 8] = b_in1[0, :D].reshape(8, 128)
    v[V_FING:V_FING + 8] = final_g.reshape(8, 128)
    v[V_LNG:V_LNG + 8] = gm_ln_g[0].reshape(8, 128)
    v[V_LNB:V_LNB + 8] = gm_ln_b[0].reshape(8, 128)
    return v


def make_in_maps(x, c, ada_w, ada_b, norm_mix_g, norm_ffn_g, w_in0, conv_w, conv_b, hg_lb, hg_gain, w_out0,
                 w_in1, b_in1, gm_ln_g, gm_ln_b, gm_ws, gm_bs, w_out1, w_ff1, w_ff2, final_g):
    f = lambda a: np.ascontiguousarray(np.asarray(a, dtype=np.float32))
    x, c, ada_w, ada_b = f(x), f(c), f(ada_w), f(ada_b)
    consts = make_consts()
    rows = np.concatenate([f(b_in1)[0, D:], f(gm_ln_b)[0], f(gm_bs)[0].reshape(-1), f(final_g)])[None, :].astype(np.float32)
    shared = {
        "rows": np.ascontiguousarray(rows), "consts": consts, "ada_w": ada_w,
        "w_in0": f(w_in0)[0], "w_out0": f(w_out0)[0], "w_in1": f(w_in1)[0], "w_out1": f(w_out1)[0],
        "w_ff1": f(w_ff1), "w_ff2": f(w_ff2), "gm_ws": f(gm_ws)[0],
    }
    maps = []
    for b in range(NCORES):
        m = dict(shared)
        m["x"] = np.ascontiguousarray(x[b])
        m["vecs"] = make_vecs(b, c, ada_b, f(norm_mix_g), f(norm_ffn_g), f(conv_w), f(conv_b), f(hg_lb), f(hg_gain),
                              f(b_in1), f(final_g), f(gm_ln_g), f(gm_ln_b))
        maps.append(m)
    return maps


_NC_CACHE = {}


def kernel(**inputs):
    maps = make_in_maps(**inputs)
    if "nc" not in _NC_CACHE:
        _NC_CACHE["nc"] = build_program()
    res = run_bass_kernel_spmd(_NC_CACHE["nc"], maps, core_ids=list(range(NCORES)))
    return np.stack([np.asarray(r["out"], dtype=np.float32) for r in res.results], axis=0)
```

```python
import contextlib
import numpy as np
import concourse.bass as bass
import concourse.mybir as mybir
from concourse.bass_utils import run_bass_kernel_spmd

F32 = mybir.dt.float32
BF16 = mybir.dt.bfloat16
AF = mybir.ActivationFunctionType
ALU = mybir.AluOpType

D = 1024
S = 4096
TT = 512
NT = S // TT
EPS = 1e-6
NCORES = 8

V_C = 0
V_ADAB = 8
V_NMG = 104
V_NFG = 120
V_CONVW = 136
V_CONVB = 148
V_HGLB = 152
V_GAIN = 164
V_BIN1U = 168
V_FING = 176
V_LNG = 184
V_LNB = 192
NVEC = 256
R_BV = 0
R_LNB = 1024
R_BS = 2048
R_FG = 2560
NROW = 3584
C_ID = 0
C_MBD = 128
C_TRIL = 256
C_CMASK = 384
NCONST = 896


import os
CUT = 0


class StopBuild(Exception):
    pass


def cut(n):
    if CUT == n:
        raise StopBuild()


class Buf:
    __slots__ = ("name", "w", "r")

    def __init__(self, name):
        self.name = name
        self.w = None
        self.r = {}


class DSem:
    def __init__(self, handle):
        self.h = handle
        self.count = 0


class Prog:
    def __init__(self, nc, stack):
        self.nc = nc
        self.stack = stack
        self.q = {}
        for name in ("pe", "act", "dve", "pool", "sp"):
            sem = stack.enter_context(nc.semaphore("q_" + name))
            self.q[name] = {"th": [], "sem": sem, "count": 0, "waited": {}}
        self.dsems = []
        self.nds = 0

    def dsem(self):
        self.nds += 1
        d = DSem(self.stack.enter_context(self.nc.semaphore("d%d" % self.nds)))
        self.dsems.append(d)
        return d

    def _wait(self, qn, tok):
        q = self.q[qn]
        sem, val = tok
        if q["waited"].get(id(sem), 0) >= val:
            return
        if qn == "pe" and sem is q["sem"]:
            return
        q["waited"][id(sem)] = val
        q["th"].append(("w", sem, val))

    def op(self, qn, fn, reads=(), writes=(), dsem=None):
        q = self.q[qn]
        for b in reads:
            if b.w is not None:
                self._wait(qn, b.w)
        for b in writes:
            for t in b.r.values():
                self._wait(qn, t)
            if b.w is not None:
                self._wait(qn, b.w)
        if dsem is None:
            q["count"] += 1
            tok = (q["sem"], q["count"])
            q["th"].append(("o", fn, q["sem"], 1))
        else:
            dsem.count += 16
            tok = (dsem.h, dsem.count)
            q["th"].append(("o", fn, dsem.h, 16))
        for b in writes:
            b.w = tok
            b.r = {}
        for b in reads:
            if b.w is not tok:
                b.r[id(tok[0])] = tok
        return tok

    def barrier(self, exclude=()):
        toks = []
        ex = set(id(d) for d in exclude)
        for qn, q in self.q.items():
            if q["count"] > 0:
                toks.append((q["sem"], q["count"]))
        for d in self.dsems:
            if id(d) in ex:
                continue
            if d.count > 0:
                toks.append((d.h, d.count))
        for qn in ("pe", "act", "dve", "pool", "sp"):
            for t in toks:
                self._wait(qn, t)

    def emit(self, qn, eng):
        for th in self.q[qn]["th"]:
            if th[0] == "w":
                eng.wait_ge(th[1], th[2])
            else:
                ins = th[1](eng)
                ins.then_inc(th[2], th[3])


def build_program(stop_after=None, ntiles=NT):
    nc = bass.Bass("TRN2", target_bir_lowering=False)
    dt = nc.dram_tensor
    x_d = dt("x", [S, D], F32, kind="ExternalInput").ap()
    vecs_d = dt("vecs", [NVEC, 128], F32, kind="ExternalInput").ap()
    rows_d = dt("rows", [1, NROW], F32, kind="ExternalInput").ap()
    consts_d = dt("consts", [128, NCONST], F32, kind="ExternalInput").ap()
    adaw_d = dt("ada_w", [2, D, 6 * D], F32, kind="ExternalInput").ap()
    win0_d = dt("w_in0", [D, 3584], F32, kind="ExternalInput").ap()
    wout0_d = dt("w_out0", [D, D], F32, kind="ExternalInput").ap()
    win1_d = dt("w_in1", [D, 2048], F32, kind="ExternalInput").ap()
    wout1_d = dt("w_out1", [D, D], F32, kind="ExternalInput").ap()
    wff1_d = dt("w_ff1", [2, D, 4096], F32, kind="ExternalInput").ap()
    wff2_d = dt("w_ff2", [2, 4096, D], F32, kind="ExternalInput").ap()
    gmws_d = dt("gm_ws", [4, 128, 128], F32, kind="ExternalInput").ap()
    out_d = dt("out", [S, D], F32, kind="ExternalOutput").ap()
    xs_d = dt("xs", [8, 128, S], F32, kind="Internal").ap()

    phases = ["M0", "F0", "M1", "F1"]
    if stop_after is not None:
        phases = phases[: phases.index(stop_after) + 1]
    do_final_norm = stop_after is None

    with contextlib.ExitStack() as stack:
        P = Prog(nc, stack)
        sb = lambda name, shape, dtype: stack.enter_context(nc.sbuf_tensor("sb_" + name, shape, dtype))
        consts = sb("consts", [128, NCONST], F32)
        vcol = sb("vcol", [128, NVEC], F32)
        identb = sb("identb", [128, 128], BF16)
        onesD = sb("onesD", [128, 128], BF16)
        ones128 = sb("ones128", [128, 128], BF16)
        maskbd = sb("maskbd", [128, 128], F32)
        modc = sb("modc", [128, 96], F32)
        s1c = sb("s1c", [128, 32], F32)
        lbc = sb("lbc", [128, 16], F32)
        epsc = sb("epsc", [128, 1], F32)
        NA = 51400
        arena = sb("arena", [128, NA], F32)
        aoff = [0]

        def areset():
            aoff[0] = 0

        def aalloc(name, shape, dtype):
            n = 1
            for d_ in shape[1:]:
                n *= d_
            words = n if dtype == F32 else (n + 1) // 2
            words = (words + 7) // 8 * 8
            o = aoff[0]
            assert o + words <= NA, (name, o, words)
            aoff[0] = o + words
            v = arena[0:shape[0], o:o + words]
            if dtype != F32:
                v = v.bitcast(dtype)
            v = v[:, 0:n]
            if len(shape) == 3:
                v = v.rearrange("p (a b) -> p a b", b=shape[2])
            elif len(shape) == 4:
                v = v.rearrange("p (a b c) -> p a b c", b=shape[2], c=shape[3])
            return v
        ps = stack.enter_context(nc.psum_tensor("ps", [128, 7, 512], F32))
        psb = stack.enter_context(nc.psum_tensor("psb", [128, 1024], BF16))
        bank = [Buf("bank%d" % i) for i in range(7)]
        bankb = Buf("bankb")
        b_consts, b_vcol, b_rows, b_small = Buf("consts"), Buf("vcol"), Buf("rows"), Buf("small")
        ident = consts[:, C_ID:C_ID + 128]
        tril = consts[:, C_TRIL:C_TRIL + 128]
        cmask = consts[:, C_CMASK:C_CMASK + 512]
        xs_bufs = [Buf("xs%d" % i) for i in range(NT)]

        def rsqrt_eps(out_ap, out_buf, in_ap, in_buf):
            P.op("act", lambda e: e.activation(out=out_ap, in_=in_ap, func=AF.Ln, bias=epsc[0:out_ap.shape[0], :], scale=1.0),
                 reads=[in_buf, b_small], writes=[out_buf])
            P.op("act", lambda e: e.activation(out=out_ap, in_=out_ap, func=AF.Exp, scale=-0.5),
                 reads=[out_buf], writes=[out_buf])

        def mm(e, out, lhsT, rhs, start, stop):
            return e.matmul(out, lhsT, rhs, start=start, stop=stop)

        d_c = P.dsem()
        P.op("sp", lambda e: e.dma_start(out=consts[:], in_=consts_d[:, :]), writes=[b_consts], dsem=d_c)
        if True:
            areset()
            vrow = aalloc("vrow", [128, 2, 128], F32)
            adaring = aalloc("adaring", [128, 3, 8, 512], BF16)
            siluc = aalloc("siluc", [128, 8], BF16)
            etmp = aalloc("etmp", [128, 16], F32)
            b_vrow = Buf("vrow")
            d_v = P.dsem()
            P.op("sp", lambda e: e.dma_start(out=vrow[:], in_=vecs_d.rearrange("(t p) f -> p t f", p=128)),
                 writes=[b_vrow], dsem=d_v)
            P.op("dve", lambda e: e.tensor_copy(out=identb[:], in_=ident), reads=[b_consts], writes=[b_small])
            P.op("dve", lambda e: e.memset(epsc[:], EPS), writes=[b_small])
            P.op("dve", lambda e: e.memset(onesD[:], 1.0 / D), writes=[b_small])
            P.op("dve", lambda e: e.memset(ones128[:], 1.0 / 128), writes=[b_small])
            P.op("dve", lambda e: e.tensor_copy(out=maskbd[:], in_=consts[:, C_MBD:C_MBD + 128]),
                 reads=[b_consts], writes=[b_small])
            for t in range(2):
                P.op("pe", lambda e, t=t: e.transpose(out=ps[:, t, 0:128], in_=vrow[:, t, :], identity=ident),
                     reads=[b_vrow, b_consts], writes=[bank[t]])
                P.op("dve", lambda e, t=t: e.tensor_copy(out=vcol[:, t * 128:(t + 1) * 128], in_=ps[:, t, 0:128]),
                     reads=[bank[t]], writes=[b_vcol])
            P.op("act", lambda e: e.activation(out=siluc[:], in_=vcol[:, V_C:V_C + 8], func=AF.Silu),
                 reads=[b_vcol], writes=[b_small])
            P.op("act", lambda e: e.activation(out=etmp[:, 0:12], in_=vcol[:, V_HGLB:V_HGLB + 12], func=AF.Exp),
                 reads=[b_vcol], writes=[b_small])
            P.op("dve", lambda e: e.tensor_tensor(out=etmp[:, 12:16], in0=etmp[:, 0:4], in1=etmp[:, 4:8], op=ALU.add),
                 reads=[b_small], writes=[b_small])
            P.op("dve", lambda e: e.tensor_tensor(out=etmp[:, 12:16], in0=etmp[:, 12:16], in1=etmp[:, 8:12], op=ALU.add),
                 reads=[b_small], writes=[b_small])
            P.op("dve", lambda e: e.reciprocal(out=etmp[:, 12:16], in_=etmp[:, 12:16]), reads=[b_small], writes=[b_small])
            P.op("dve", lambda e: e.tensor_tensor(out=lbc[:, 0:4], in0=etmp[:, 0:4], in1=etmp[:, 12:16], op=ALU.mult),
                 reads=[b_small], writes=[b_small])
            P.op("dve", lambda e: e.tensor_scalar(out=lbc[:, 4:8], in0=lbc[:, 0:4], scalar1=-1.0, scalar2=1.0,
                                                  op0=ALU.mult, op1=ALU.add), reads=[b_small], writes=[b_small])
            P.op("dve", lambda e: e.tensor_scalar(out=lbc[:, 8:12], in0=lbc[:, 4:8], scalar1=-1.0, scalar2=None,
                                                  op0=ALU.mult), reads=[b_small], writes=[b_small])
            aslots = [Buf("ada%d" % i) for i in range(3)]
            adsem = [P.dsem() for _ in range(3)]
            b_ada = bank[2]
            n = 0
            for l in range(2):
                wv = adaw_d[l].rearrange("(kc p) e -> p kc e", p=128)
                for ch in range(12):
                    sl = n % 3
                    P.op("pool", lambda e, sl=sl, wv=wv, ch=ch: e.dma_start(
                        out=adaring[:, sl], in_=wv[:, :, ch * 512:(ch + 1) * 512]),
                        writes=[aslots[sl]], dsem=adsem[sl])

                    def f(e, sl=sl, col0=l * 48 + ch * 4):
                        ins = None
                        for es in range(4):
                            for kc in range(8):
                                ins = mm(e, ps[:, 2, col0 + es:col0 + es + 1],
                                         adaring[:, sl, kc, es * 128:(es + 1) * 128], siluc[:, kc:kc + 1],
                                         kc == 0, kc == 7)
                        return ins
                    P.op("pe", f, reads=[aslots[sl], b_small], writes=[b_ada])
                    n += 1
            P.op("dve", lambda e: e.tensor_tensor(out=modc[:], in0=ps[:, 2, 0:96], in1=vcol[:, V_ADAB:V_ADAB + 96],
                                                  op=ALU.add), reads=[b_ada, b_vcol], writes=[b_small])
            for l in range(2):
                P.op("dve", lambda e, l=l: e.scalar_tensor_tensor(
                    out=s1c[:, l * 16:l * 16 + 8], in0=modc[:, l * 48 + 8:l * 48 + 16], scalar=1.0,
                    in1=vcol[:, V_NMG + l * 8:V_NMG + l * 8 + 8], op0=ALU.add, op1=ALU.mult),
                    reads=[b_small, b_vcol], writes=[b_small])
                P.op("dve", lambda e, l=l: e.scalar_tensor_tensor(
                    out=s1c[:, l * 16 + 8:l * 16 + 16], in0=modc[:, l * 48 + 32:l * 48 + 40], scalar=1.0,
                    in1=vcol[:, V_NFG + l * 8:V_NFG + l * 8 + 8], op0=ALU.add, op1=ALU.mult),
                    reads=[b_small, b_vcol], writes=[b_small])
            P.barrier()

        def mod(l, grp, c):
            return modc[:, l * 48 + grp * 8 + c:l * 48 + grp * 8 + c + 1]

        def scol(l, which, c):
            o = l * 16 + which * 8 + c
            return s1c[:, o:o + 1]

        def vc(base, i):
            return vcol[:, base + i:base + i + 1]

        wslot = [Buf("wslot%d" % k) for k in range(16)]
        wdsem = [P.dsem() for _ in range(16)]
        wtoks = []
        wissued = set()

        def slot_view(k, kind):
            v = arena[:, k * 2048:(k + 1) * 2048].bitcast(BF16)
            return v.rearrange("p (a b) -> p a b", b=(512 if kind == "k" else D))

        def phase_chunks(ph):
            out = []
            if ph == "M0":
                wv = win0_d.rearrange("(kc p) e -> p kc e", p=128)
                wov = wout0_d.rearrange("(kc p) e -> p kc e", p=128)
                for i in range(7):
                    out.append((("M0", "in", i), i, "k", wv[:, :, i * 512:(i + 1) * 512]))
                for i in range(2):
                    out.append((("M0", "out", i), 7 + i, "k", wov[:, :, i * 512:(i + 1) * 512]))
            elif ph == "M1":
                wv = win1_d.rearrange("(kc p) e -> p kc e", p=128)
                wov = wout1_d.rearrange("(kc p) e -> p kc e", p=128)
                for i in range(4):
                    out.append((("M1", "in", i), i, "k", wv[:, :, i * 512:(i + 1) * 512]))
                for i in range(2):
                    out.append((("M1", "out", i), 4 + i, "k", wov[:, :, i * 512:(i + 1) * 512]))
            else:
                lf = int(ph[1])
                w1v = wff1_d[lf].rearrange("(kc p) e -> p kc e", p=128)
                w2v = wff2_d[lf].rearrange("(fc p) d -> p fc d", p=128)
                for i in range(8):
                    out.append(((ph, "w1", i), i, "k", w1v[:, :, i * 512:(i + 1) * 512]))
                for i in range(8):
                    out.append(((ph, "w2", i), 8 + i, "w", w2v[:, i * 4:(i + 1) * 4, :]))
            return out

        def issue_weights(ph, max_slot):
            for key, k, kind, src in phase_chunks(ph):
                if k >= max_slot or key in wissued:
                    continue
                wissued.add(key)
                if len(wtoks) >= 4:
                    P._wait("pool", wtoks[-4])
                wtoks.append(P.op("pool", lambda e, k=k, kind=kind, src=src: e.dma_start(out=slot_view(k, kind), in_=src),
                                  writes=[wslot[k]], dsem=wdsem[k]))

        def load_weight_chunks(wsb, src_fn, nchunks, name):
            bufs = []
            toks = []
            for i in range(nchunks):
                b = Buf("%s%d" % (name, i))
                d = P.dsem()
                if i >= 4:
                    P._wait("pool", toks[i - 4])
                toks.append(P.op("pool", lambda e, i=i: e.dma_start(out=wsb[:, i], in_=src_fn(i)), writes=[b], dsem=d))
                bufs.append(b)
            return bufs

        def norm_and_modulate(xT, xTb, sqt, sqb, tmp, tmpb, hT, hb, invB, b_inv, b_norm_bank, nbank, l, which,
                              sq_src=None):
            for c in range(8):
                src_ap, src_b = (xT[:, c, :], xTb[c]) if sq_src is None else sq_src[c]
                P.op("act", lambda e, c=c, src_ap=src_ap: e.activation(out=sqt[:, c % 2, :], in_=src_ap, func=AF.Square),
                     reads=[src_b], writes=[sqb[c % 2]])
                P.op("pe", lambda e, c=c: mm(e, ps[:, nbank, :], onesD[:], sqt[:, c % 2, :], c == 0, c == 7),
                     reads=[sqb[c % 2], b_small], writes=[b_norm_bank])
            rsqrt_eps(invB[:], b_inv, ps[:, nbank, :], b_norm_bank)
            if hT is None:
                return
            shift_grp = 0 if which == 0 else 3
            ntmp = tmp.shape[1]
            for c in range(8):
                P.op("dve", lambda e, c=c: e.scalar_tensor_tensor(
                    out=tmp[:, c % ntmp, :], in0=xT[:, c, :], scalar=scol(l, which, c), in1=invB[:],
                    op0=ALU.mult, op1=ALU.mult), reads=[xTb[c], b_inv, b_small], writes=[tmpb[c % ntmp]])
                P.op("act", lambda e, c=c: e.activation(out=hT[:, c, :], in_=tmp[:, c % ntmp, :], func=AF.Identity,
                                                        bias=mod(l, shift_grp, c), scale=1.0),
                     reads=[tmpb[c % ntmp], b_small], writes=[hb[c]])

        def store_xs(xT, xTb, ti, dsem):
            P.op("sp", lambda e: e.dma_start(out=xs_d[:, :, ti * TT:(ti + 1) * TT].rearrange("c p t -> p c t"),
                                             in_=xT[:]), reads=xTb, writes=[xs_bufs[ti]], dsem=dsem)

        def load_xs(xT, xTb, ti, dsem):
            P.op("sp", lambda e: e.dma_start(out=xT[:], in_=xs_d[:, :, ti * TT:(ti + 1) * TT].rearrange("c p t -> p c t")),
                 reads=[xs_bufs[ti]], writes=xTb, dsem=dsem)

        def phase_m0(next_ph=None):
            l = 0
            with contextlib.ExitStack() as st:
                areset()
                a = aalloc
                win = a("m0_win", [128, 7, 8, 512], BF16)
                wout = a("m0_wout", [128, 2, 8, 512], BF16)
                xin_raw = a("m0_xin", [128, 4 * D], F32)
                xin = xin_raw.rearrange("p (g d) -> p g d", d=D)
                ftB = xin_raw[:, 0:3072].rearrange("p (a b) -> p a b", b=TT)
                kdB = xin_raw[:, 3072:3328].bitcast(BF16).rearrange("p (a b) -> p a b", b=TT)
                cref2B = xin_raw[:, 3328:3344].rearrange("p (a b) -> p a b", b=2)
                Rg = [a("m0_R0", [128, 8, TT], F32), a("m0_R1", [128, 8, TT], F32)]
                RgB = [[Buf("R%d_%d" % (r_, c)) for c in range(8)] for r_ in range(2)]
                sqt = a("m0_sq", [128, 2, TT], BF16)
                tmp = a("m0_tmp", [128, 1, TT], F32)
                hT = a("m0_h", [128, 8, TT], BF16)
                invB = a("m0_inv", [128, TT], F32)
                ac = a("m0_ac", [128, 2, TT], BF16)
                z = a("m0_z", [128, 4, TT + 2], F32)

                ycat = a("m0_ycat", [128, 8, TT], BF16)

                kA = a("m0_kA", [128, 4, TT], BF16)
                kB = a("m0_kB", [128, 4, TT], BF16)
                qs = a("m0_qs", [128, 4, TT], BF16)
                cref2 = a("m0_cref2", [128, 8, 2], F32)
                kd = a("m0_kd", [128, 1, TT], BF16)
                kdT = a("m0_kdT", [128, 4, 512], BF16)
                qe = a("m0_qe", [128, 4, TT], BF16)
                vt = a("m0_v", [128, 4, 512], BF16)
                sg = a("m0_sg", [128, 4, TT], BF16)
                elb = a("m0_el", [128, 4, 8], F32)
                scT = a("m0_scT", [128, 2, 4, 128], BF16)
                osq = a("m0_osq", [128, 2, 512], BF16)
                invo = a("m0_invo", [128, 512], F32)
                t1 = a("m0_t1", [128, 512], F32)
                Sst = a("m0_S", [128, 512], F32)
                Sb = a("m0_Sb", [128, 3, 512], BF16)
                oi = a("m0_oi", [128, 512], F32)
                osum = a("m0_osum", [128, 2, 512], F32)
                b_oi = Buf("oi"); b_osum = [Buf("osum0"), Buf("osum1")]

                issue_weights("M0", 16)
                winb = wslot[0:7]
                woutb = wslot[7:9]

                b_xin = Buf("xin")
                sqb = [Buf("sq0"), Buf("sq1")]; tmpb = [Buf("tmp0"), Buf("tmp1")]
                hb = [Buf("h%d" % c) for c in range(8)]; b_inv = Buf("inv")
                acb = [Buf("ac0"), Buf("ac1")]; zb = [Buf("z%d" % j) for j in range(4)]
                ycb = [Buf("yc%d" % j) for j in range(8)]
                kAb = [Buf("kA%d" % j) for j in range(4)]; kBb = [Buf("kB%d" % j) for j in range(4)]
                qsb = [Buf("qs%d" % j) for j in range(4)]; b_cref = Buf("cref"); kdb = [Buf("kd0")]
                ftbB = [Buf("ftB%d" % j) for j in range(6)]; kdbB = [Buf("kdB0")]; b_crefB = Buf("crefB")
                b_kdT = Buf("kdT"); qeb = [Buf("qe%d" % j) for j in range(4)]
                b_v = Buf("v"); sgb = [Buf("sg%d" % j) for j in range(4)]; b_el = Buf("el")
                scTb = [Buf("scT0"), Buf("scT1")]; osqb = [Buf("osq0"), Buf("osq1")]
                b_invo = Buf("invo"); b_t1 = Buf("t1"); b_S = Buf("S"); Sbb = [Buf("Sb%d" % i) for i in range(3)]
                d_xin = P.dsem(); d_st = P.dsem()
                B_main = [bank[0], bank[1]]; B_x = [bank[2], bank[3]]
                B_norm, B_U, B_v = bank[4], bank[5], bank[6]
                B_sc, B_o = 2, 3

                P.op("dve", lambda e: e.memset(z[:], 0.0), writes=zb)
                P.op("dve", lambda e: e.memset(Sst[:], 0.0), writes=[b_S])
                P.op("dve", lambda e: e.memset(cref2[:], 0.0), writes=[b_cref])
                P.op("dve", lambda e: e.memset(Sb[:, 0, :], 0.0), writes=[Sbb[0]])
                kglob = [0]

                def load_xin(ti):
                    P.op("sp", lambda e: e.dma_start(
                        out=xin[:], in_=x_d[ti * TT:(ti + 1) * TT, :].rearrange("(g p) d -> p g d", p=128)),
                        writes=[b_xin], dsem=d_xin)

                def proj_fm(bk, wslot_ap, wbuf, col0):
                    def f(e):
                        ins = None
                        for kc in range(8):
                            ins = mm(e, ps[:, bk, :], wslot_ap[:, kc, col0:col0 + 128], hT[:, kc, :], kc == 0, kc == 7)
                        return ins
                    P.op("pe", f, reads=[wbuf] + hb, writes=[bank[bk]])

                cut(20)
                load_xin(0)
                nmain = [0]

                def next_main():
                    nmain[0] += 1
                    return nmain[0] % 2

                nx = [0]

                def next_x():
                    nx[0] += 1
                    return 2 + nx[0] % 2

                def A_gen(xTn, xTbn):
                    for c in range(8):
                        bk = c % 2
                        def f(e, c=c, bk=bk):
                            ins = None
                            for g in range(4):
                                ins = e.transpose(out=ps[:, bk, g * 128:(g + 1) * 128],
                                                  in_=xin[:, g, c * 128:(c + 1) * 128], identity=ident)
                            return ins
                        P.op("pe", f, reads=[b_xin, b_consts], writes=[bank[bk]])
                        P.op("dve", lambda e, c=c, bk=bk: e.tensor_copy(out=xTn[:, c, :], in_=ps[:, bk, :]),
                             reads=[bank[bk]], writes=[xTbn[c]])
                        yield
                    norm_and_modulate(xTn, xTbn, sqt, sqb, tmp, tmpb, hT, hb, invB, b_inv, B_norm, 4, l, 0)
                    yield

                for _ in A_gen(Rg[0], RgB[0]):
                    pass
                for ti in range(ntiles):
                    cur = ti % 2
                    xT, xTb = Rg[cur], RgB[cur]
                    ft, ftb = Rg[1 - cur][:, 0:6, :], RgB[1 - cur][0:6]
                    acc, accb = Rg[1 - cur][:, 6:8, :], RgB[1 - cur][6:8]
                    cut(2)
                    def conv_body(j, acc=acc, accb=accb):
                        bk = next_x()
                        proj_fm(bk, win[:, 1], winb[1], j * 128)
                        P.op("act", lambda e, j=j, bk=bk: e.activation(out=ac[:, j % 2, :], in_=ps[:, bk, :], func=AF.Copy),
                             reads=[bank[bk]], writes=[acb[j % 2]])
                        yield
                        bk = next_x()
                        proj_fm(bk, win[:, 2], winb[2], j * 128)
                        P.op("dve", lambda e, j=j, bk=bk: e.tensor_tensor(out=z[:, j, 2:TT + 2], in0=ps[:, bk, :],
                                                                          in1=ac[:, j % 2, :], op=ALU.mult),
                             reads=[bank[bk], acb[j % 2]], writes=[zb[j]])
                        P.op("act", lambda e, j=j: e.activation(out=acc[:, j % 2, :], in_=z[:, j, 2:TT + 2],
                                                                func=AF.Identity, bias=vc(V_CONVB, j),
                                                                scale=vc(V_CONVW, 2 * 4 + j)),
                             reads=[zb[j], b_vcol], writes=[accb[j % 2]])
                        for tap in (1, 0):
                            P.op("dve", lambda e, j=j, tap=tap: e.scalar_tensor_tensor(
                                out=acc[:, j % 2, :], in0=z[:, j, tap:tap + TT], scalar=vc(V_CONVW, tap * 4 + j),
                                in1=acc[:, j % 2, :], op0=ALU.mult, op1=ALU.add),
                                reads=[zb[j], accb[j % 2], b_vcol], writes=[accb[j % 2]])
                        P.op("act", lambda e, j=j: e.activation(out=z[:, j, 0:2], in_=z[:, j, TT:TT + 2], func=AF.Copy),
                             reads=[zb[j]], writes=[zb[j]])
                        yield
                        bk = next_x()
                        proj_fm(bk, win[:, 0], winb[0], j * 128)
                        P.op("dve", lambda e, j=j, bk=bk: e.tensor_tensor(out=ycat[:, j, :], in0=ps[:, bk, :],
                                                                          in1=acc[:, j % 2, :], op=ALU.mult),
                             reads=[bank[bk], accb[j % 2]], writes=[ycb[j]])
                        yield
                    cut(3)
                    def gates_body(j, S_ft, S_ftb, S_kd, S_kdb, S_cref2, S_bcref, S_bf, S_bq):
                        bk = S_bf
                        proj_fm(bk, win[:, 4], winb[4], j * 128)
                        P.op("act", lambda e, bk=bk: e.activation(out=S_ft[:, 0, :], in_=ps[:, bk, :], func=AF.Sigmoid),
                             reads=[bank[bk]], writes=[S_ftb[0]])
                        yield
                        P.op("act", lambda e, j=j: e.activation(out=S_ft[:, 1, :], in_=S_ft[:, 0, :], func=AF.Ln,
                                                                bias=lbc[:, j:j + 1], scale=lbc[:, 4 + j:5 + j]),
                             reads=[S_ftb[0], b_small], writes=[S_ftb[1]])
                        yield
                        P.op("dve", lambda e, j=j: e.tensor_scalar(out=S_ft[:, 2, :], in0=S_ft[:, 0, :],
                                                                   scalar1=lbc[:, 8 + j:9 + j], scalar2=lbc[:, 4 + j:5 + j],
                                                                   op0=ALU.mult, op1=ALU.add),
                             reads=[S_ftb[0], b_small], writes=[S_ftb[2]])
                        yield
                        P.op("dve", lambda e: e.tensor_tensor_scan(out=S_ft[:, 3, :], data0=cmask, data1=S_ft[:, 1, :],
                                                                   initial=0.0, op0=ALU.mult, op1=ALU.add),
                             reads=[S_ftb[1], b_consts], writes=[S_ftb[3]])
                        yield
                        cum3 = S_ft[:, 3, :].rearrange("p (c t) -> p c t", t=64)
                        cum4 = S_ft[:, 3, :].rearrange("p (c h t) -> p c h t", h=2, t=32)
                        P.op("act", lambda e: e.activation(out=S_ft[:, 4, :], in_=S_ft[:, 3, :], func=AF.Exp),
                             reads=[S_ftb[3]], writes=[S_ftb[4]])
                        yield
                        P.op("dve", lambda e, j=j: e.tensor_copy(
                            out=elb[:, j, :], in_=S_ft[:, 4, :].rearrange("p (c t) -> p c t", t=64)[:, :, 63]),
                            reads=[S_ftb[4]], writes=[b_el])
                        yield
                        P.op("dve", lambda e, cum3=cum3: e.tensor_copy(out=S_cref2[:, :, 1], in_=cum3[:, :, 31]),
                             reads=[S_ftb[3]], writes=[S_bcref])
                        yield
                        P.op("dve", lambda e, cum3=cum3: e.tensor_tensor(
                            out=S_ft[:, 0, :].rearrange("p (c t) -> p c t", t=64), in0=cum3,
                            in1=cum3[:, :, 63:64].to_broadcast([128, 8, 64]), op=ALU.subtract),
                            reads=[S_ftb[3]], writes=[S_ftb[0]])
                        yield
                        P.op("act", lambda e: e.activation(out=S_ft[:, 0, :], in_=S_ft[:, 0, :], func=AF.Exp, scale=-1.0),
                             reads=[S_ftb[0]], writes=[S_ftb[0]])
                        yield
                        P.op("dve", lambda e: e.tensor_tensor(out=S_kd[:, 0, :], in0=S_ft[:, 2, :], in1=S_ft[:, 0, :], op=ALU.mult),
                             reads=[S_ftb[2], S_ftb[0]], writes=[S_kdb[0]])
                        yield
                        P.op("dve", lambda e, cum4=cum4: e.tensor_tensor(
                            out=S_ft[:, 1, :].rearrange("p (c h t) -> p c h t", h=2, t=32), in0=cum4,
                            in1=S_cref2[:].unsqueeze(3).to_broadcast([128, 8, 2, 32]), op=ALU.subtract),
                            reads=[S_ftb[3], S_bcref], writes=[S_ftb[1]])
                        yield
                        P.op("act", lambda e: e.activation(out=S_ft[:, 1, :], in_=S_ft[:, 1, :], func=AF.Exp),
                             reads=[S_ftb[1]], writes=[S_ftb[1]])
                        yield
                        P.op("dve", lambda e: e.tensor_scalar(out=S_ft[:, 5, :], in0=S_ft[:, 3, :], scalar1=-85.0, scalar2=None,
                                                              op0=ALU.max), reads=[S_ftb[3]], writes=[S_ftb[5]])
                        yield
                        P.op("act", lambda e: e.activation(out=S_ft[:, 5, :], in_=S_ft[:, 5, :], func=AF.Exp, scale=-1.0),
                             reads=[S_ftb[5]], writes=[S_ftb[5]])
                        yield
                        P.op("dve", lambda e, j=j: e.tensor_tensor(out=kA[:, j, :], in0=S_ft[:, 2, :], in1=S_ft[:, 5, :],
                                                                   op=ALU.mult),
                             reads=[S_ftb[2], S_ftb[5]], writes=[kAb[j]])
                        yield
                        P.op("dve", lambda e, cum3=cum3: e.tensor_tensor(
                            out=S_ft[:, 0, :].rearrange("p (c t) -> p c t", t=64), in0=cum3,
                            in1=S_cref2[:, :, 1:2].to_broadcast([128, 8, 64]), op=ALU.subtract),
                            reads=[S_ftb[3], S_bcref], writes=[S_ftb[0]])
                        yield
                        P.op("dve", lambda e: e.tensor_scalar(out=S_ft[:, 0, :], in0=S_ft[:, 0, :], scalar1=-85.0, scalar2=None,
                                                              op0=ALU.max), reads=[S_ftb[0]], writes=[S_ftb[0]])
                        yield
                        P.op("act", lambda e: e.activation(out=S_ft[:, 0, :], in_=S_ft[:, 0, :], func=AF.Exp, scale=-1.0),
                             reads=[S_ftb[0]], writes=[S_ftb[0]])
                        yield
                        P.op("dve", lambda e, j=j: e.tensor_tensor(out=kB[:, j, :], in0=S_ft[:, 2, :], in1=S_ft[:, 0, :],
                                                                   op=ALU.mult),
                             reads=[S_ftb[2], S_ftb[0]], writes=[kBb[j]])
                        yield
                        bk = S_bq
                        proj_fm(bk, win[:, 3], winb[3], j * 128)
                        P.op("dve", lambda e, j=j, bk=bk: e.tensor_tensor(out=qe[:, j, :], in0=ps[:, bk, :],
                                                                          in1=S_ft[:, 4, :], op=ALU.mult),
                             reads=[bank[bk], S_ftb[4]], writes=[qeb[j]])
                        yield
                        P.op("dve", lambda e, j=j, bk=bk: e.tensor_tensor(out=qs[:, j, :], in0=ps[:, bk, :],
                                                                          in1=S_ft[:, 1, :], op=ALU.mult),
                             reads=[bank[bk], S_ftb[1]], writes=[qsb[j]])
                        yield
                        def f(e, j=j):
                            ins = None
                            for g in range(4):
                                ins = e.transpose(out=psb[:, g * 128:(g + 1) * 128],
                                                  in_=S_kd[:, 0, g * 128:(g + 1) * 128], identity=identb[:])
                            return ins
                        P.op("pe", f, reads=[S_kdb[0], b_small], writes=[bankb])
                        P.op("act", lambda e, j=j: e.activation(
                            out=kdT[:, :, j * 128:(j + 1) * 128],
                            in_=psb[:, 0:512].rearrange("p (g k) -> p g k", k=128), func=AF.Copy),
                            reads=[bankb], writes=[b_kdT])
                        yield
                    def bg_body(j):
                        bk = next_x()
                        proj_fm(bk, win[:, 6], winb[6], j * 128)
                        P.op("act", lambda e, j=j, bk=bk: e.activation(out=sg[:, j, :], in_=ps[:, bk, :], func=AF.Silu),
                             reads=[bank[bk]], writes=[sgb[j]])
                        yield
                    def bi_body(g):
                        def f(e, g=g):
                            ins = None
                            for kc in range(8):
                                ins = mm(e, ps[:, 6, :], hT[:, kc, g * 128:(g + 1) * 128], win[:, 5, kc, :], kc == 0, kc == 7)
                            return ins
                        P.op("pe", f, reads=[winb[5]] + hb, writes=[B_v])
                        P.op("act", lambda e, g=g: e.activation(out=vt[:, g, :], in_=ps[:, 6, :], func=AF.Copy),
                             reads=[B_v], writes=[b_v])
                        yield

                    def chain_gens(gens):
                        for g_ in gens:
                            yield from g_
                    def bg_all():
                        for j in range(4):
                            for _ in bg_body(j):
                                pass
                        yield
                    gx = chain_gens([conv_body(j) for j in range(4)] + [bg_all()]
                                    + [bi_body(g) for g in range(4)])
                    setA = (ft, ftb, kd, kdb, cref2, b_cref, 0, 1)
                    setB = (ftB, ftbB, kdB, kdbB, cref2B, b_crefB, 4, 5)

                    def rr2(ga, gb):
                        la = lb = True
                        while la or lb:
                            if la:
                                try:
                                    next(ga)
                                except StopIteration:
                                    la = False
                            if lb:
                                try:
                                    next(gb)
                                except StopIteration:
                                    lb = False
                            yield
                    P.op("dve", lambda e: e.memset(cref2B[:], 0.0), writes=[b_xin, b_crefB, kdbB[0]] + ftbB)
                    gy = chain_gens([rr2(gates_body(0, *setA), gates_body(1, *setB)),
                                     rr2(gates_body(2, *setA), gates_body(3, *setB))])
                    alive_x = alive_y = True
                    while alive_x or alive_y:
                        for _ in range(2):
                            if alive_y:
                                try:
                                    next(gy)
                                except StopIteration:
                                    alive_y = False
                        if alive_x:
                            try:
                                next(gx)
                            except StopIteration:
                                alive_x = False
                    P.op("dve", lambda e: e.memset(cref2B[:, 0, 0:1], 0.0), writes=[b_xin, b_crefB, kdbB[0]] + ftbB)
                    if ti + 1 < ntiles:
                        load_xin(ti + 1)
                    cut(4)
                    def z1(g):
                        def f(e, g=g):
                            ins = None
                            for j in range(4):
                                for I in range(4):
                                    src = kA if I % 2 == 0 else kB
                                    ins = mm(e, ps[:, B_sc, j * 128 + I * 32:j * 128 + I * 32 + 32],
                                             src[:, j, g * 128:(g + 1) * 128],
                                             qs[:, j, g * 128 + I * 32:g * 128 + I * 32 + 32], True, True)
                            return ins
                        P.op("pe", f, reads=kAb + kBb + qsb, writes=[bank[B_sc]])
                        P.op("dve", lambda e, g=g: e.tensor_tensor(
                            out=scT[:, g % 2], in0=ps[:, B_sc, :].rearrange("p (j t) -> p j t", t=128),
                            in1=maskbd[:].unsqueeze(1).to_broadcast([128, 4, 128]), op=ALU.mult),
                            reads=[bank[B_sc], b_small], writes=[scTb[g % 2]])

                        k0 = kglob[0]

                        def do_U(half, g=g, k0=k0):
                            nxt = (k0 + half + 1) % 3
                            def f(e, half=half, g=g):
                                ins = None
                                r0 = half * 64
                                for j in range(4):
                                    ins = mm(e, ps[:, 5, j * 128:(j + 1) * 128], kdT[r0:r0 + 64, g, j * 128:(j + 1) * 128],
                                             vt[r0:r0 + 64, g, j * 128:(j + 1) * 128], True, True)
                                return ins
                            P.op("pe", f, reads=[b_kdT, b_v], writes=[B_U])
                            ci = g * 2 + half
                            for j in range(4):
                                P.op("dve", lambda e, j=j, ci=ci: e.scalar_tensor_tensor(
                                    out=Sst[:, j * 128:(j + 1) * 128], in0=Sst[:, j * 128:(j + 1) * 128],
                                    scalar=elb[:, j, ci:ci + 1], in1=ps[:, 5, j * 128:(j + 1) * 128],
                                    op0=ALU.mult, op1=ALU.add), reads=[b_S, b_el, B_U], writes=[b_S])
                            P.op("act", lambda e, nxt=nxt: e.activation(out=Sb[:, nxt, :], in_=Sst[:], func=AF.Copy),
                                 reads=[b_S], writes=[Sbb[nxt]])

                        do_U(0)

                        def f(e, g=g):
                            ins = None
                            for j in range(4):
                                ins = mm(e, ps[:, B_o, j * 128:(j + 1) * 128], vt[:, g, j * 128:(j + 1) * 128],
                                         scT[:, g % 2, j, :], True, True)
                            return ins
                        P.op("pe", f, reads=[b_v, scTb[g % 2]], writes=[bank[B_o]])

                        def f(e, g=g, k0=k0):
                            ins = None
                            for j in range(4):
                                for half in range(2):
                                    cur = (k0 + half) % 3
                                    t0 = g * 128 + half * 64
                                    ins = mm(e, ps[:, 6, j * 128 + half * 64:j * 128 + half * 64 + 64],
                                             Sb[:, cur, j * 128:(j + 1) * 128], qe[:, j, t0:t0 + 64], True, True)
                            return ins
                        P.op("pe", f, reads=[Sbb[k0 % 3], Sbb[(k0 + 1) % 3]] + qeb, writes=[B_v])
                        P.op("act", lambda e: e.activation(out=oi[:], in_=ps[:, 6, :], func=AF.Copy),
                             reads=[B_v], writes=[b_oi])
                        P.op("dve", lambda e, g=g: e.tensor_tensor(out=osum[:, g % 2, :], in0=ps[:, B_o, :], in1=oi[:], op=ALU.add),
                             reads=[bank[B_o], b_oi], writes=[b_osum[g % 2]])
                        do_U(1)
                        kglob[0] += 2

                    def z2(g):
                        P.op("act", lambda e, g=g: e.activation(out=osq[:, g % 2, :], in_=osum[:, g % 2, :], func=AF.Square),
                             reads=[b_osum[g % 2]], writes=[osqb[g % 2]])
                        P.op("pe", lambda e, g=g: mm(e, ps[:, 4, :], ones128[:], osq[:, g % 2, :], True, True),
                             reads=[osqb[g % 2], b_small], writes=[B_norm])
                        rsqrt_eps(invo[:], b_invo, ps[:, 4, :], B_norm)
                        P.op("dve", lambda e, g=g: e.tensor_tensor(out=t1[:], in0=osum[:, g % 2, :], in1=invo[:], op=ALU.mult),
                             reads=[b_osum[g % 2], b_invo], writes=[b_t1])
                        for j in range(4):
                            P.op("dve", lambda e, j=j, g=g: e.scalar_tensor_tensor(
                                out=ycat[:, 4 + j, g * 128:(g + 1) * 128], in0=t1[:, j * 128:(j + 1) * 128],
                                scalar=vc(V_GAIN, j), in1=sg[:, j, g * 128:(g + 1) * 128], op0=ALU.mult, op1=ALU.mult),
                                reads=[b_t1, sgb[j], b_vcol], writes=[ycb[4 + j]])

                    if ti + 1 < ntiles:
                        ag = A_gen(Rg[1 - cur], RgB[1 - cur])
                    else:
                        ag = iter(())

                    def a_steps(n):
                        for _ in range(n):
                            try:
                                next(ag)
                            except StopIteration:
                                return
                    z1(0); a_steps(2); z1(1); a_steps(2); z2(0); a_steps(2); z1(2); a_steps(2); z2(1); a_steps(1)
                    z1(3); z2(2); z2(3); a_steps(8)
                    cut(5)
                    for c in range(8):
                        bk = next_main()
                        def f(e, c=c, bk=bk):
                            ins = None
                            for k in range(8):
                                ins = mm(e, ps[:, bk, :], wout[:, c // 4, k, (c % 4) * 128:(c % 4) * 128 + 128],
                                         ycat[:, k, :], k == 0, k == 7)
                            return ins
                        P.op("pe", f, reads=[woutb[c // 4]] + ycb, writes=[bank[bk]])
                        P.op("dve", lambda e, c=c, bk=bk, xT=xT: e.scalar_tensor_tensor(
                            out=xT[:, c, :], in0=ps[:, bk, :], scalar=mod(l, 2, c), in1=xT[:, c, :],
                            op0=ALU.mult, op1=ALU.add), reads=[bank[bk], xTb[c], b_small], writes=[xTb[c]])
                    store_xs(xT, xTb, ti, d_st)
                if next_ph is not None:
                    issue_weights(next_ph, 9)
                P.barrier(exclude=wdsem)

        def phase_ffn(l, final, next_ph=None):
            with contextlib.ExitStack() as st:
                areset()
                a = aalloc
                pre = "f%d_" % l
                w1 = a(pre + "w1", [128, 8, 8, 512], BF16)
                w2 = a(pre + "w2", [128, 8, 4, D], BF16)
                xT2 = a(pre + "xT", [128, 2, 8, TT], F32)
                sqt = a(pre + "sq", [128, 2, TT], BF16)
                tmp = a(pre + "tmp", [128, 1 if final else 2, TT], F32)
                hT = a(pre + "h", [128, 8, TT], BF16)
                invB = a(pre + "inv", [128, TT], F32)
                hid = a(pre + "hid", [128, 16, TT], BF16)
                rt = a(pre + "rt", [128, 2, TT], BF16)
                yout = a(pre + "yout", [128, 1, D], F32) if final else None
                if final:
                    fgB = a(pre + "fgB", [128, D], F32)
                    fst = a(pre + "fst", [128, 16], F32)
                    b_fg = Buf("fgB"); b_fst = Buf("fst")
                    d_fg = P.dsem()
                    P.op("sp", lambda e: e.dma_start(
                        out=fgB[:], in_=rows_d[0:1, R_FG:R_FG + D].partition_broadcast(128)),
                        writes=[b_fg], dsem=d_fg)

                issue_weights("F%d" % l, 16)
                w1b = wslot[0:8]
                w2b = wslot[8:16]

                xTb2 = [[Buf("xT%d_%d" % (s_, c)) for c in range(8)] for s_ in range(2)]
                sqb = [Buf("sq0"), Buf("sq1")]; tmpb = [Buf("tmp0"), Buf("tmp1")]
                hb = [Buf("h%d" % c) for c in range(8)]; b_inv = Buf("inv")
                hidb = [Buf("hid%d" % i) for i in range(16)]; rtb = [Buf("rt0"), Buf("rt1")]
                youtb = [Buf("yout0")]
                d_ld = [P.dsem(), P.dsem()]; d_st = [P.dsem(), P.dsem()]; d_out = [P.dsem()]
                nmain = [0]

                mainbanks = [0, 1, 2, 3] if final else [0, 1, 2, 3, 5, 6]

                def next_main():
                    nmain[0] += 1
                    return mainbanks[nmain[0] % len(mainbanks)]

                def ff1(half):
                    for fcl in range(16):
                        fc = half * 16 + fcl
                        bk = next_main()
                        def f(e, fc=fc, bk=bk):
                            ins = None
                            for kc in range(8):
                                ins = mm(e, ps[:, bk, :], w1[:, fc // 4, kc, (fc % 4) * 128:(fc % 4) * 128 + 128],
                                         hT[:, kc, :], kc == 0, kc == 7)
                            return ins
                        P.op("pe", f, reads=[w1b[fc // 4]] + hb, writes=[bank[bk]])
                        P.op("act", lambda e, fc=fc, bk=bk: e.activation(out=rt[:, fc % 2, :], in_=ps[:, bk, :], func=AF.Relu),
                             reads=[bank[bk]], writes=[rtb[fc % 2]])
                        P.op("dve", lambda e, fc=fc, fcl=fcl: e.tensor_tensor(out=hid[:, fcl, :], in0=rt[:, fc % 2, :],
                                                                              in1=rt[:, fc % 2, :], op=ALU.mult),
                             reads=[rtb[fc % 2]], writes=[hidb[fcl]])
                        yield

                def ff2(half, xT, xTb):
                    for c in range(8):
                        bk = next_main()
                        def f(e, c=c, bk=bk):
                            ins = None
                            for fcl in range(16):
                                fc = half * 16 + fcl
                                ins = mm(e, ps[:, bk, :], w2[:, fc // 4, fc % 4, c * 128:(c + 1) * 128], hid[:, fcl, :],
                                         fcl == 0, fcl == 15)
                            return ins
                        P.op("pe", f, reads=w2b[half * 4:(half + 1) * 4] + hidb, writes=[bank[bk]])
                        P.op("dve", lambda e, c=c, bk=bk: e.scalar_tensor_tensor(
                            out=xT[:, c, :], in0=ps[:, bk, :], scalar=mod(l, 5, c), in1=xT[:, c, :],
                            op0=ALU.mult, op1=ALU.add), reads=[bank[bk], xTb[c], b_small], writes=[xTb[c]])

                load_xs(xT2[:, 0], xTb2[0], 0, d_ld[0])
                norm_and_modulate(xT2[:, 0], xTb2[0], sqt, sqb, tmp, tmpb, hT, hb, invB, b_inv, bank[4], 4, l, 1)
                def final_gen(ti, xT, xTb):
                    for g in range(4):
                        for hh in range(2):
                            bk = 5 + hh
                            def f(e, g=g, hh=hh, bk=bk, xT=xT):
                                ins = None
                                for cc in range(4):
                                    c = hh * 4 + cc
                                    ins = e.transpose(out=ps[:, bk, cc * 128:(cc + 1) * 128],
                                                      in_=xT[:, c, g * 128:(g + 1) * 128], identity=ident)
                                return ins
                            P.op("pe", f, reads=xTb + [b_consts], writes=[bank[bk]])
                        if do_final_norm:
                            for hh in range(2):
                                P.op("dve", lambda e, hh=hh: e.bn_stats(out=fst[:, hh * 6:hh * 6 + 6], in_=ps[:, 5 + hh, :]),
                                     reads=[bank[5 + hh]], writes=[b_fst])
                            P.op("dve", lambda e: e.bn_aggr(out=fst[:, 12:14], in_=fst[:, 0:12]), reads=[b_fst], writes=[b_fst])
                            P.op("dve", lambda e: e.scalar_tensor_tensor(out=fst[:, 14:15], in0=fst[:, 12:13],
                                                                         scalar=fst[:, 12:13], in1=fst[:, 13:14],
                                                                         op0=ALU.mult, op1=ALU.add),
                                 reads=[b_fst], writes=[b_fst])
                            rsqrt_eps(fst[:, 15:16], b_fst, fst[:, 14:15], b_fst)
                            for hh in range(2):
                                P.op("dve", lambda e, hh=hh: e.scalar_tensor_tensor(
                                    out=yout[:, 0, hh * 512:(hh + 1) * 512], in0=ps[:, 5 + hh, :], scalar=fst[:, 15:16],
                                    in1=fgB[:, hh * 512:(hh + 1) * 512], op0=ALU.mult, op1=ALU.mult),
                                    reads=[bank[5 + hh], b_fst, b_fg], writes=[youtb[0]])
                        else:
                            for hh in range(2):
                                P.op("dve", lambda e, hh=hh: e.tensor_copy(out=yout[:, 0, hh * 512:(hh + 1) * 512],
                                                                           in_=ps[:, 5 + hh, :]),
                                     reads=[bank[5 + hh]], writes=[youtb[0]])
                        r0 = ti * TT + g * 128
                        P.op("sp", lambda e, g=g, r0=r0: e.dma_start(out=out_d[r0:r0 + 128, :], in_=yout[:, 0, :]),
                             reads=[youtb[0]], dsem=d_out[0])
                        yield

                pending = None
                for ti in range(ntiles):
                    cur = ti % 2
                    xT, xTb = xT2[:, cur], xTb2[cur]
                    if not final and ti + 1 < ntiles:
                        load_xs(xT2[:, 1 - cur], xTb2[1 - cur], ti + 1, d_ld[1 - cur])
                    k = 0
                    for _ in ff1(0):
                        k += 1
                        if pending is not None and k % 4 == 0:
                            try:
                                next(pending)
                            except StopIteration:
                                pending = None
                    if pending is not None:
                        for _ in pending:
                            pass
                        pending = None
                    if final and ti + 1 < ntiles:
                        load_xs(xT2[:, 1 - cur], xTb2[1 - cur], ti + 1, d_ld[1 - cur])
                    ff2(0, xT, xTb)
                    for _ in ff1(1):
                        pass
                    if ti + 1 < ntiles:
                        norm_and_modulate(xT2[:, 1 - cur], xTb2[1 - cur], sqt, sqb, tmp, tmpb, hT, hb, invB, b_inv,
                                          bank[4], 4, l, 1)
                    ff2(1, xT, xTb)
                    if not final:
                        store_xs(xT, xTb, ti, d_st[cur])
                        continue
                    pending = final_gen(ti, xT, xTb)
                if pending is not None:
                    for _ in pending:
                        pass
                if next_ph is not None:
                    issue_weights(next_ph, 16)
                P.barrier(exclude=wdsem)

        def phase_m1(next_ph=None):
            l = 1
            with contextlib.ExitStack() as st:
                areset()
                a = aalloc
                win = a("m1_win", [128, 4, 8, 512], BF16)
                wout = a("m1_wout", [128, 2, 8, 512], BF16)
                xT2 = a("m1_xT", [128, 2, 8, TT], F32)
                sqt = a("m1_sq", [128, 2, TT], BF16)
                tmp = a("m1_tmp", [128, 2, TT], F32)
                hT = a("m1_h", [128, 8, TT], BF16)
                invB = a("m1_inv", [128, TT], F32)
                uT = a("m1_u", [128, 8, TT], BF16)
                gv = a("m1_gv", [128, 2, D], F32)
                vn = a("m1_vn", [128, 2, D], BF16)
                st6 = a("m1_st6", [128, 2, 2, 6], F32)
                mv = a("m1_mv", [128, 2, 4], F32)
                t1 = a("m1_t1", [128, 2, 512], F32)
                ymix = a("m1_ymix", [128, 8, TT], BF16)
                RB = a("m1_RB", [128, 8, 128], F32)
                ws32 = a("m1_ws32", [128, 4, 128], F32)
                wmT32 = a("m1_wmT32", [128, 4, 128], F32)
                wmT = a("m1_wmT", [128, 4, 128], BF16)
                onesbb = a("m1_onesbb", [128, 128], BF16)
                bvB = a("m1_bvB", [128, D], F32)
                bsB = a("m1_bsB", [128, 512], F32)
                gpre = a("m1_gpre", [128, 2, 2, 512], F32)
                gpreb = [[Buf("gpre%d%d" % (i_, h_)) for h_ in range(2)] for i_ in range(2)]
                b_stg = [Buf("stg0"), Buf("stg1")]
                xTb2 = [[Buf("xT%d_%d" % (s_, c)) for c in range(8)] for s_ in range(2)]
                d_ld2 = [P.dsem(), P.dsem()]; d_st2 = [P.dsem(), P.dsem()]
                d_r = P.dsem()
                P.op("sp", lambda e: e.dma_start(out=bvB[:], in_=rows_d[0:1, R_BV:R_BV + D].partition_broadcast(128)),
                     writes=[b_rows], dsem=d_r)
                P.op("sp", lambda e: e.dma_start(out=bsB[:], in_=rows_d[0:1, R_BS:R_BS + 512].partition_broadcast(128)),
                     writes=[b_rows], dsem=d_r)

                issue_weights("M1", 16)
                winb = wslot[0:4]
                woutb = wslot[4:6]

                xTb = [Buf("xT%d" % c) for c in range(8)]
                sqb = [Buf("sq0"), Buf("sq1")]; tmpb = [Buf("tmp0"), Buf("tmp1")]
                hb = [Buf("h%d" % c) for c in range(8)]; b_inv = Buf("inv")
                ub = [Buf("u%d" % c) for c in range(8)]
                gvb = [Buf("gv0"), Buf("gv1")]; vnb = [Buf("vn0"), Buf("vn1")]
                b_st = Buf("st6"); t1b = [Buf("t10"), Buf("t11")]
                ymb = [Buf("ym%d" % c) for c in range(8)]
                b_m1c = Buf("m1consts"); b_ws = Buf("ws32")
                d_ld = P.dsem(); d_st = P.dsem(); d_ws = P.dsem()

                P.op("sp", lambda e: e.dma_start(out=ws32[:], in_=gmws_d.rearrange("g t s -> t g s")),
                     writes=[b_ws], dsem=d_ws)
                P.op("dve", lambda e: e.tensor_tensor(out=ws32[:], in0=ws32[:],
                                                      in1=tril.unsqueeze(1).to_broadcast([128, 4, 128]), op=ALU.mult),
                     reads=[b_ws, b_consts], writes=[b_ws])
                def f(e):
                    ins = None
                    for g in range(4):
                        ins = e.transpose(out=ps[:, 0, g * 128:(g + 1) * 128], in_=ws32[:, g, :], identity=ident)
                    return ins
                P.op("pe", f, reads=[b_ws, b_consts], writes=[bank[0]])
                P.op("dve", lambda e: e.tensor_copy(out=wmT32[:], in_=ps[:, 0, :].rearrange("p (g t) -> p g t", t=128)),
                     reads=[bank[0]], writes=[b_m1c])
                P.op("act", lambda e: e.activation(out=wmT[:], in_=ps[:, 0, :].rearrange("p (g t) -> p g t", t=128),
                                                   func=AF.Copy), reads=[bank[0]], writes=[b_m1c])
                P.op("dve", lambda e: e.memset(onesbb[:], 1.0), writes=[b_m1c])
                P.op("pe", lambda e: mm(e, ps[:, 1, :], onesbb[:], wmT[:].rearrange("p g t -> p (g t)"), True, True),
                     reads=[b_m1c], writes=[bank[1]])
                for ec in range(8):
                    g = ec // 2
                    P.op("dve", lambda e, ec=ec, g=g: e.scalar_tensor_tensor(
                        out=RB[:, ec, :], in0=ps[:, 1, g * 128:(g + 1) * 128], scalar=vc(V_LNB, ec),
                        in1=bsB[:, g * 128:(g + 1) * 128], op0=ALU.mult, op1=ALU.add),
                        reads=[bank[1], b_vcol, b_rows], writes=[b_m1c])

                nmain = [0]

                def next_main():
                    nmain[0] += 1
                    return nmain[0] % 2

                def u_proj(ec):
                    bk = next_main()
                    def f(e, ec=ec, bk=bk):
                        ins = None
                        for kc in range(8):
                            ins = mm(e, ps[:, bk, :], win[:, ec // 4, kc, (ec % 4) * 128:(ec % 4) * 128 + 128],
                                     hT[:, kc, :], kc == 0, kc == 7)
                        return ins
                    P.op("pe", f, reads=[winb[ec // 4]] + hb, writes=[bank[bk]])
                    P.op("act", lambda e, ec=ec, bk=bk: e.activation(out=uT[:, ec, :], in_=ps[:, bk, :],
                                                                     func=AF.Gelu_apprx_tanh, bias=vc(V_BIN1U, ec),
                                                                     scale=1.0),
                         reads=[bank[bk], b_vcol], writes=[ub[ec]])

                def stage_a(g):
                    for hh in range(2):
                        bk = 2 + hh
                        def f(e, g=g, hh=hh, bk=bk):
                            ins = None
                            for kc in range(8):
                                ins = mm(e, ps[:, bk, :], hT[:, kc, g * 128:(g + 1) * 128], win[:, 2 + hh, kc, :], kc == 0, kc == 7)
                            return ins
                        P.op("pe", f, reads=[winb[2 + hh]] + hb, writes=[bank[bk]])
                        P.op("dve", lambda e, g=g, hh=hh, bk=bk: e.tensor_tensor(
                            out=gpre[:, g % 2, hh, :], in0=ps[:, bk, :], in1=bvB[:, hh * 512:(hh + 1) * 512], op=ALU.add),
                            reads=[bank[bk], b_rows], writes=[gpreb[g % 2][hh]])
                        P.op("act", lambda e, g=g, hh=hh: e.activation(
                            out=gv[:, g % 2, hh * 512:(hh + 1) * 512], in_=gpre[:, g % 2, hh, :], func=AF.Gelu_apprx_tanh),
                            reads=[gpreb[g % 2][hh]], writes=[gvb[g % 2]])

                def stage_b(g):
                    s6, mvg, bst = st6[:, g % 2], mv[:, g % 2], b_stg[g % 2]
                    for hh in range(2):
                        P.op("dve", lambda e, g=g, hh=hh, s6=s6: e.bn_stats(out=s6[:, hh, :], in_=gv[:, g % 2, hh * 512:(hh + 1) * 512]),
                             reads=[gvb[g % 2]], writes=[bst])
                    P.op("dve", lambda e, s6=s6, mvg=mvg: e.bn_aggr(out=mvg[:, 0:2], in_=s6.rearrange("p a b -> p (a b)")),
                         reads=[bst], writes=[bst])
                    rsqrt_eps(mvg[:, 2:3], bst, mvg[:, 1:2], bst)
                    P.op("dve", lambda e, mvg=mvg: e.scalar_tensor_tensor(out=mvg[:, 3:4], in0=mvg[:, 0:1], scalar=-1.0,
                                                                          in1=mvg[:, 2:3], op0=ALU.mult, op1=ALU.mult),
                         reads=[bst], writes=[bst])
                    P.op("act", lambda e, g=g, mvg=mvg: e.activation(out=vn[:, g % 2, :], in_=gv[:, g % 2, :], func=AF.Identity,
                                                                     bias=mvg[:, 3:4], scale=mvg[:, 2:3]),
                         reads=[gvb[g % 2], bst], writes=[vnb[g % 2]])

                def stage_c(g):
                    for hh in range(2):
                        bk = 5 + hh
                        def f(e, g=g, hh=hh, bk=bk):
                            ins = None
                            for cc in range(4):
                                ec = hh * 4 + cc
                                ins = mm(e, ps[:, bk, cc * 128:(cc + 1) * 128], vn[:, g % 2, ec * 128:(ec + 1) * 128],
                                         wmT[:, ec // 2, :], True, True)
                            return ins
                        P.op("pe", f, reads=[vnb[g % 2], b_m1c], writes=[bank[bk]])
                        for cc in range(4):
                            ec = hh * 4 + cc
                            P.op("dve", lambda e, cc=cc, ec=ec, hh=hh, bk=bk: e.scalar_tensor_tensor(
                                out=t1[:, hh, cc * 128:(cc + 1) * 128], in0=ps[:, bk, cc * 128:(cc + 1) * 128],
                                scalar=vc(V_LNG, ec), in1=RB[:, ec, :], op0=ALU.mult, op1=ALU.add),
                                reads=[bank[bk], b_vcol, b_m1c], writes=[t1b[hh]])
                        P.op("dve", lambda e, g=g, hh=hh: e.tensor_tensor(
                            out=ymix[:, hh * 4:(hh + 1) * 4, g * 128:(g + 1) * 128],
                            in0=t1[:, hh, :].rearrange("p (c t) -> p c t", t=128),
                            in1=uT[:, hh * 4:(hh + 1) * 4, g * 128:(g + 1) * 128], op=ALU.mult),
                            reads=[t1b[hh]] + ub[hh * 4:(hh + 1) * 4], writes=ymb[hh * 4:(hh + 1) * 4])

                cut(31)
                load_xs(xT2[:, 0], xTb2[0], 0, d_ld2[0])
                norm_and_modulate(xT2[:, 0], xTb2[0], sqt, sqb, tmp, tmpb, hT, hb, invB, b_inv, bank[4], 4, l, 0)
                for ti in range(ntiles):
                    cur = ti % 2
                    xT, xTb = xT2[:, cur], xTb2[cur]
                    if ti + 1 < ntiles:
                        load_xs(xT2[:, 1 - cur], xTb2[1 - cur], ti + 1, d_ld2[1 - cur])
                    u_proj(0); u_proj(1); stage_a(0)
                    u_proj(2); u_proj(3); stage_a(1)
                    u_proj(4); u_proj(5); stage_b(0)
                    u_proj(6); u_proj(7); stage_a(2)
                    stage_b(1); stage_c(0); stage_a(3); stage_b(2); stage_c(1); stage_b(3); stage_c(2); stage_c(3)
                    if ti + 1 < ntiles:
                        norm_and_modulate(xT2[:, 1 - cur], xTb2[1 - cur], sqt, sqb, tmp, tmpb, hT, hb, invB, b_inv,
                                          bank[4], 4, l, 0)
                    for c in range(8):
                        bk = next_main()
                        def f(e, c=c, bk=bk):
                            ins = None
                            for k in range(8):
                                ins = mm(e, ps[:, bk, :], wout[:, c // 4, k, (c % 4) * 128:(c % 4) * 128 + 128],
                                         ymix[:, k, :], k == 0, k == 7)
                            return ins
                        P.op("pe", f, reads=[woutb[c // 4]] + ymb, writes=[bank[bk]])
                        P.op("dve", lambda e, c=c, bk=bk, xT=xT: e.scalar_tensor_tensor(
                            out=xT[:, c, :], in0=ps[:, bk, :], scalar=mod(l, 2, c), in1=xT[:, c, :],
                            op0=ALU.mult, op1=ALU.add), reads=[bank[bk], xTb[c], b_small], writes=[xTb[c]])
                    store_xs(xT, xTb, ti, d_st2[cur])
                if next_ph is not None:
                    issue_weights(next_ph, 6)
                P.barrier(exclude=wdsem)

        for pi, ph in enumerate(phases if CUT != 1 else []):
          nxt = phases[pi + 1] if pi + 1 < len(phases) else None
          try:
            if ph == "M0":
                phase_m0(nxt)
            elif ph == "F0":
                phase_ffn(0, final=(phases[-1] == "F0"), next_ph=nxt)
            elif ph == "M1":
                phase_m1(nxt)
            elif ph == "F1":
                cut(35)
                phase_ffn(1, final=True)
          except StopBuild:
            phases = ["F0"]
            break
        if phases[-1] in ("M0", "M1"):
            phase_dump = True
        else:
            phase_dump = False
        if phase_dump:
            if True:
                areset()
                dx = aalloc("dump_x", [128, 8, TT], F32)
                dy = aalloc("dump_y", [128, 2, D], F32)
                dxb = [Buf("dx")]
                dyb = [Buf("dy0"), Buf("dy1")]
                d1 = P.dsem(); d2 = [P.dsem(), P.dsem()]
                for ti in range(ntiles):
                    load_xs(dx, dxb, ti, d1)
                    for g in range(4):
                        for hh in range(2):
                            bk = 5 + hh
                            def f(e, g=g, hh=hh, bk=bk):
                                ins = None
                                for cc in range(4):
                                    c = hh * 4 + cc
                                    ins = e.transpose(out=ps[:, bk, cc * 128:(cc + 1) * 128],
                                                      in_=dx[:, c, g * 128:(g + 1) * 128], identity=ident)
                                return ins
                            P.op("pe", f, reads=dxb + [b_consts], writes=[bank[bk]])
                            P.op("act", lambda e, g=g, hh=hh, bk=bk: e.activation(
                                out=dy[:, g % 2, hh * 512:(hh + 1) * 512], in_=ps[:, bk, :], func=AF.Copy),
                                reads=[bank[bk]], writes=[dyb[g % 2]])
                        r0 = ti * TT + g * 128
                        P.op("sp", lambda e, g=g, r0=r0: e.dma_start(out=out_d[r0:r0 + 128, :], in_=dy[:, g % 2, :]),
                             reads=[dyb[g % 2]], dsem=d2[g % 2])
                P.barrier()
        P.barrier()

        with nc.Block() as block:
            @block.tensor
            def _(e):
                P.emit("pe", e)

            @block.scalar
            def _(e):
                P.emit("act", e)

            @block.vector
            def _(e):
                P.emit("dve", e)

            @block.gpsimd
            def _(e):
                P.emit("pool", e)

            @block.sync
            def _(e):
                P.emit("sp", e)
    return nc


def make_consts():
    c = np.zeros((128, NCONST), np.float32)
    c[:, C_ID:C_ID + 128] = np.eye(128, dtype=np.float32)
    s = np.arange(128)[:, None]
    t = np.arange(128)[None, :]
    c[:, C_MBD:C_MBD + 128] = ((s // 64 == t // 64) & (s <= t)).astype(np.float32)
    c[:, C_TRIL:C_TRIL + 128] = (t <= s).astype(np.float32)
    cm = np.ones((512,), np.float32)
    cm[::64] = 0.0
    c[:, C_CMASK:C_CMASK + 512] = cm[None, :]
    return c


def make_vecs(b, c, ada_b, norm_mix_g, norm_ffn_g, conv_w, conv_b, hg_lb, hg_gain, b_in1, final_g, gm_ln_g, gm_ln_b):
    v = np.zeros((NVEC, 128), np.float32)
    v[V_C:V_C + 8] = c[b].reshape(8, 128)
    v[V_ADAB:V_ADAB + 48] = ada_b[0].reshape(48, 128)
    v[V_ADAB + 48:V_ADAB + 96] = ada_b[1].reshape(48, 128)
    v[V_NMG:V_NMG + 16] = norm_mix_g.reshape(16, 128)
    v[V_NFG:V_NFG + 16] = norm_ffn_g.reshape(16, 128)
    v[V_CONVW:V_CONVW + 12] = conv_w[0].reshape(12, 128)
    v[V_CONVB:V_CONVB + 4] = conv_b[0].reshape(4, 128)
    v[V_HGLB:V_HGLB + 12] = hg_lb.reshape(12, 128)
    v[V_GAIN:V_GAIN + 4] = hg_gain[0].reshape(4, 128)
    v[V_BIN1U:V_BIN1U + 8] = b_in1[0, :D].reshape(8, 128)
    v[V_FING:V_FING + 8] = final_g.reshape(8, 128)
    v[V_LNG:V_LNG + 8] = gm_ln_g[0].reshape(8, 128)
    v[V_LNB:V_LNB + 8] = gm_ln_b[0].reshape(8, 128)
    return v


def make_in_maps(x, c, ada_w, ada_b, norm_mix_g, norm_ffn_g, w_in0, conv_w, conv_b, hg_lb, hg_gain, w_out0,
                 w_in1, b_in1, gm_ln_g, gm_ln_b, gm_ws, gm_bs, w_out1, w_ff1, w_ff2, final_g):
    f = lambda a: np.ascontiguousarray(np.asarray(a, dtype=np.float32))
    x, c, ada_w, ada_b = f(x), f(c), f(ada_w), f(ada_b)
    consts = make_consts()
    rows = np.concatenate([f(b_in1)[0, D:], f(gm_ln_b)[0], f(gm_bs)[0].reshape(-1), f(final_g)])[None, :].astype(np.float32)
    shared = {
        "rows": np.ascontiguousarray(rows), "consts": consts, "ada_w": ada_w,
        "w_in0": f(w_in0)[0], "w_out0": f(w_out0)[0], "w_in1": f(w_in1)[0], "w_out1": f(w_out1)[0],
        "w_ff1": f(w_ff1), "w_ff2": f(w_ff2), "gm_ws": f(gm_ws)[0],
    }
    maps = []
    for b in range(NCORES):
        m = dict(shared)
        m["x"] = np.ascontiguousarray(x[b])
        m["vecs"] = make_vecs(b, c, ada_b, f(norm_mix_g), f(norm_ffn_g), f(conv_w), f(conv_b), f(hg_lb), f(hg_gain),
                              f(b_in1), f(final_g), f(gm_ln_g), f(gm_ln_b))
        maps.append(m)
    return maps


_NC_CACHE = {}


def kernel(**inputs):
    maps = make_in_maps(**inputs)
    if "nc" not in _NC_CACHE:
        _NC_CACHE["nc"] = build_program()
    res = run_bass_kernel_spmd(_NC_CACHE["nc"], maps, core_ids=list(range(NCORES)))
    return np.stack([np.asarray(r["out"], dtype=np.float32) for r in res.results], axis=0)
```
